# Optimizing a Trainium2 kernel written in Bass

```python
import functools
import jax, jax.numpy as jnp
from jax import lax
import numpy as np

D_MODEL = 1024
BATCH = 8
SEQ = 8192
DEPTH = 4
DEC_BATCH = 32
DEC_SEQ = 16
PAST_LEN = 1024

CHUNK = 64
N_LAYERS_A = DEPTH // 2
N_LAYERS_B = DEPTH - N_LAYERS_A
HEAD_DIM = 64
N_HEADS = D_MODEL // HEAD_DIM
N_KV_HEADS = 4
GROUP = N_HEADS // N_KV_HEADS
WINDOW_A = 128
LEFT_CHUNKS_A = WINDOW_A // CHUNK
LEFT_CHUNKS_B = 8
WINDOW_B = LEFT_CHUNKS_B * CHUNK
MAX_REL = 128
N_REL = 2 * MAX_REL + 1
D_FF = 4 * D_MODEL
EPS = 1e-6
NEG = -1e30

kernel_name = "yoco_chunk_stream_encoder_step"


def rmsnorm(x, g):
    xf = x.astype(jnp.float32)
    y = xf * lax.rsqrt(jnp.mean(xf * xf, axis=-1, keepdims=True) + EPS)
    return (y * g.astype(jnp.float32)).astype(x.dtype)


def band_mask(q_pos, k_pos, left_chunks):
    qc = q_pos[:, None] // CHUNK
    kc = k_pos[None, :] // CHUNK
    return (k_pos[None, :] >= 0) & (kc <= qc) & (kc >= qc - left_chunks)


def alibi_bias(q_pos, k_pos, slopes):
    dist = jnp.abs(q_pos[:, None] - k_pos[None, :]).astype(jnp.float32)
    bias = -slopes[:, None, None] * dist[None]
    return jnp.where(band_mask(q_pos, k_pos, LEFT_CHUNKS_A)[None], bias, NEG)


def relpos_bias(q_pos, k_pos, table):
    rel = jnp.clip(q_pos[:, None] - k_pos[None, :], -MAX_REL, MAX_REL) + MAX_REL
    bias = table.astype(jnp.float32)[:, rel]
    return jnp.where(band_mask(q_pos, k_pos, LEFT_CHUNKS_B)[None], bias, NEG)


def attend(q, k, v, bias, sink):
    b, tq = q.shape[:2]
    qg = q.reshape(b, tq, N_KV_HEADS, GROUP, HEAD_DIM)
    s = jnp.einsum('bqhgd,bkhd->bhgqk', qg, k).astype(jnp.float32) * (HEAD_DIM ** -0.5)
    s = s + bias.reshape(N_KV_HEADS, GROUP, tq, -1)
    if sink is None:
        p = jax.nn.softmax(s, axis=-1)
    else:
        sk = sink.astype(jnp.float32).reshape(N_KV_HEADS, GROUP, 1, 1)
        m = jnp.maximum(jnp.max(s, axis=-1, keepdims=True), sk)
        e = jnp.exp(s - m)
        p = e / (jnp.sum(e, axis=-1, keepdims=True) + jnp.exp(sk - m))
    o = jnp.einsum('bhgqk,bkhd->bqhgd', p.astype(v.dtype), v)
    return o.reshape(b, tq, N_HEADS * HEAD_DIM)


def band_attention_prompt(q, k, v, left_chunks, bias_fn, sink):
    b, s = q.shape[:2]
    n_chunks = s // CHUNK
    pad = left_chunks * CHUNK
    band = pad + CHUNK
    k_pad = jnp.pad(k, ((0, 0), (pad, 0), (0, 0), (0, 0)))
    v_pad = jnp.pad(v, ((0, 0), (pad, 0), (0, 0), (0, 0)))
    q_chunks = q.reshape(b, n_chunks, CHUNK, N_HEADS, HEAD_DIM).swapaxes(0, 1)

    def one_chunk(args):
        c, q_c = args
        start = c * CHUNK
        k_c = lax.dynamic_slice_in_dim(k_pad, start, band, axis=1)
        v_c = lax.dynamic_slice_in_dim(v_pad, start, band, axis=1)
        q_pos = start + jnp.arange(CHUNK)
        k_pos = start - pad + jnp.arange(band)
        return attend(q_c, k_c, v_c, bias_fn(q_pos, k_pos), sink)

    out = lax.map(one_chunk, (jnp.arange(n_chunks), q_chunks))
    return out.swapaxes(0, 1).reshape(b, s, N_HEADS * HEAD_DIM)


def project_a(x, g_norm, w_qkv, g_q, g_k):
    b, t = x.shape[:2]
    qkv = rmsnorm(x, g_norm) @ w_qkv
    q, k, v = jnp.split(qkv, [N_HEADS * HEAD_DIM, (N_HEADS + N_KV_HEADS) * HEAD_DIM], axis=-1)
    q = rmsnorm(q.reshape(b, t, N_HEADS, HEAD_DIM), g_q)
    k = rmsnorm(k.reshape(b, t, N_KV_HEADS, HEAD_DIM), g_k)
    return q, k, v.reshape(b, t, N_KV_HEADS, HEAD_DIM)


def project_q_b(x, g_norm, w_q, g_q):
    b, t = x.shape[:2]
    return rmsnorm((rmsnorm(x, g_norm) @ w_q).reshape(b, t, N_HEADS, HEAD_DIM), g_q)


def shared_kv(x, g_norm, w_kv, g_k):
    b, t = x.shape[:2]
    kv = (rmsnorm(x, g_norm) @ w_kv).reshape(b, t, 2, N_KV_HEADS, HEAD_DIM)
    return rmsnorm(kv[:, :, 0], g_k), kv[:, :, 1]


def mlp(x, g_norm, w_up, w_down):
    return jnp.square(jax.nn.relu(rmsnorm(x, g_norm) @ w_up)) @ w_down


def setup_inputs(seed: int = 0) -> dict:
    key = jax.random.key(seed)
    ks = jax.random.split(key, 24)
    nrm = lambda k, shape, scale: jax.random.normal(k, shape, jnp.float32) * scale
    len_a = min(WINDOW_A, PAST_LEN)
    len_b = min(WINDOW_B, PAST_LEN)
    qkv_w = (N_HEADS + 2 * N_KV_HEADS) * HEAD_DIM
    return {
        "x_prompt": nrm(ks[0], (BATCH, SEQ, D_MODEL), 1.0),
        "x_sample": nrm(ks[1], (DEC_BATCH, DEC_SEQ, D_MODEL), 1.0),
        "cache_k_a": nrm(ks[2], (N_LAYERS_A, DEC_BATCH, len_a, N_KV_HEADS, HEAD_DIM), 1.0),
        "cache_v_a": nrm(ks[3], (N_LAYERS_A, DEC_BATCH, len_a, N_KV_HEADS, HEAD_DIM), 1.0),
        "cache_k_b": nrm(ks[4], (DEC_BATCH, len_b, N_KV_HEADS, HEAD_DIM), 1.0),
        "cache_v_b": nrm(ks[5], (DEC_BATCH, len_b, N_KV_HEADS, HEAD_DIM), 1.0),
        "g_attn": 1.0 + nrm(ks[6], (DEPTH, D_MODEL), 0.05),
        "g_mlp": 1.0 + nrm(ks[7], (DEPTH, D_MODEL), 0.05),
        "w_qkv_a": nrm(ks[8], (N_LAYERS_A, D_MODEL, qkv_w), D_MODEL ** -0.5),
        "g_q_a": 1.0 + nrm(ks[9], (N_LAYERS_A, HEAD_DIM), 0.05),
        "g_k_a": 1.0 + nrm(ks[10], (N_LAYERS_A, HEAD_DIM), 0.05),
        "sink_a": nrm(ks[11], (N_LAYERS_A, N_HEADS), 0.5),
        "w_o_a": nrm(ks[12], (N_LAYERS_A, N_HEADS * HEAD_DIM, D_MODEL), (N_HEADS * HEAD_DIM) ** -0.5),
        "g_kv": 1.0 + nrm(ks[13], (D_MODEL,), 0.05),
        "w_kv": nrm(ks[14], (D_MODEL, 2 * N_KV_HEADS * HEAD_DIM), D_MODEL ** -0.5),
        "g_k_b": 1.0 + nrm(ks[15], (HEAD_DIM,), 0.05),
        "w_q_b": nrm(ks[16], (N_LAYERS_B, D_MODEL, N_HEADS * HEAD_DIM), D_MODEL ** -0.5),
        "g_q_b": 1.0 + nrm(ks[17], (N_LAYERS_B, HEAD_DIM), 0.05),
        "rel_bias_b": nrm(ks[18], (N_LAYERS_B, N_HEADS, N_REL), 0.5),
        "w_o_b": nrm(ks[19], (N_LAYERS_B, N_HEADS * HEAD_DIM, D_MODEL), (N_HEADS * HEAD_DIM) ** -0.5),
        "w_up": nrm(ks[20], (DEPTH, D_MODEL, D_FF), D_MODEL ** -0.5),
        "w_down": nrm(ks[21], (DEPTH, D_FF, D_MODEL), D_FF ** -0.5),
    }


def reference(x_prompt, x_sample, cache_k_a, cache_v_a, cache_k_b, cache_v_b,
              g_attn, g_mlp, w_qkv_a, g_q_a, g_k_a, sink_a, w_o_a,
              g_kv, w_kv, g_k_b, w_q_b, g_q_b, rel_bias_b, w_o_b, w_up, w_down):
    slopes = 2.0 ** (-8.0 * jnp.arange(1, N_HEADS + 1, dtype=jnp.float32) / N_HEADS)
    hp, hs = x_prompt, x_sample
    seq = x_prompt.shape[1]
    t_new = x_sample.shape[1]
    pos_s = PAST_LEN + jnp.arange(t_new)
    len_a = cache_k_a.shape[2]
    len_b = cache_k_b.shape[1]
    kpos_a = jnp.concatenate([PAST_LEN - len_a + jnp.arange(len_a), pos_s])
    kpos_b = jnp.concatenate([PAST_LEN - len_b + jnp.arange(len_b), pos_s])
    keep_a = min(WINDOW_A, seq)
    keep_b = min(WINDOW_B, seq)
    alibi_fn = functools.partial(alibi_bias, slopes=slopes)
    ka_p, va_p, ka_s, va_s = [], [], [], []

    for layer in range(DEPTH):
        if layer < N_LAYERS_A:
            i = layer
            qp, kp, vp = project_a(hp, g_attn[layer], w_qkv_a[i], g_q_a[i], g_k_a[i])
            ap = band_attention_prompt(qp, kp, vp, LEFT_CHUNKS_A, alibi_fn, sink_a[i])
            ka_p.append(kp[:, seq - keep_a:])
            va_p.append(vp[:, seq - keep_a:])
            qs, ks_, vs_ = project_a(hs, g_attn[layer], w_qkv_a[i], g_q_a[i], g_k_a[i])
            k_all = jnp.concatenate([cache_k_a[i], ks_], axis=1)
            v_all = jnp.concatenate([cache_v_a[i], vs_], axis=1)
            as_ = attend(qs, k_all, v_all, alibi_bias(pos_s, kpos_a, slopes), sink_a[i])
            ka_s.append(ks_)
            va_s.append(vs_)
            hp = hp + ap @ w_o_a[i]
            hs = hs + as_ @ w_o_a[i]
        else:
            j = layer - N_LAYERS_A
            if j == 0:
                kb_p, vb_p = shared_kv(hp, g_kv, w_kv, g_k_b)
                kb_s, vb_s = shared_kv(hs, g_kv, w_kv, g_k_b)
                kb_all = jnp.concatenate([cache_k_b, kb_s], axis=1)
                vb_all = jnp.concatenate([cache_v_b, vb_s], axis=1)
            rel_fn = functools.partial(relpos_bias, table=rel_bias_b[j])
            qp = project_q_b(hp, g_attn[layer], w_q_b[j], g_q_b[j])
            ap = band_attention_prompt(qp, kb_p, vb_p, LEFT_CHUNKS_B, rel_fn, None)
            qs = project_q_b(hs, g_attn[layer], w_q_b[j], g_q_b[j])
            as_ = attend(qs, kb_all, vb_all, relpos_bias(pos_s, kpos_b, rel_bias_b[j]), None)
            hp = hp + ap @ w_o_b[j]
            hs = hs + as_ @ w_o_b[j]
        hp = hp + mlp(hp, g_mlp[layer], w_up[layer], w_down[layer])
        hs = hs + mlp(hs, g_mlp[layer], w_up[layer], w_down[layer])

    new_k_a_prompt = jnp.stack(ka_p)
    new_v_a_prompt = jnp.stack(va_p)
    new_k_a_sample = jnp.stack(ka_s)
    new_v_a_sample = jnp.stack(va_s)
    new_k_b_prompt = kb_p[:, seq - keep_b:]
    new_v_b_prompt = vb_p[:, seq - keep_b:]
    return (hp, hs, new_k_a_prompt, new_v_a_prompt, new_k_b_prompt, new_v_b_prompt,
            new_k_a_sample, new_v_a_sample, kb_s, vb_s)
```

```python
import numpy as np
import concourse.bass as bass
import concourse.mybir as mybir
from concourse.bass_utils import run_bass_kernel_spmd

F32 = mybir.dt.float32
BF16 = mybir.dt.bfloat16
ALU = mybir.AluOpType
AF = mybir.ActivationFunctionType

NCORES = 8
D = 1024
SEQ = 8192
T = 512
NT = SEQ // T
TS = 64
NSTREAM = 4
DEC = 16
EPS = 1e-6
NW = 4
NT_RUN = NT
import os as _os
DBG_NOLAG = bool(_os.environ.get('DBG_NOLAG'))
DBG_NOMERGE = bool(_os.environ.get('DBG_NOMERGE'))
DBG_NOPAIR = bool(_os.environ.get('DBG_NOPAIR'))
DO_SAMPLE = True

SLOPES = [2.0 ** (-8.0 * (h + 1) / 16.0) for h in range(16)]


class Op:
    __slots__ = ("eng", "fn", "deps", "chan", "ordinal", "target", "idx", "bulk", "raw")

    def __init__(self, eng, fn, chan=None, bulk=False):
        self.eng = eng
        self.fn = fn
        self.deps = ()
        self.chan = chan
        self.ordinal = 0
        self.target = False
        self.idx = 0
        self.bulk = bulk
        self.raw = ()


class Sched:
    def __init__(self):
        self.ops = []
        self.lastw = {}
        self.readers = {}
        self.chan_count = {}

    def add(self, eng, fn, reads=(), writes=(), chan=None, bulk=False, excl=()):
        i = len(self.ops)
        op = Op(eng, fn, chan, bulk)
        deps = set()
        lastw, readers = self.lastw, self.readers
        for k in reads:
            w = lastw.get(k)
            if w is not None:
                deps.add(w)
        if excl:
            real = set(deps)
            for k in writes:
                w = lastw.get(k)
                if w is not None:
                    real.add(w)
                r = readers.get(k)
                if r:
                    real.update(r.values())
            op.raw = frozenset(real)
            writes = list(writes) + list(excl)
        else:
            op.raw = None
        for k in writes:
            w = lastw.get(k)
            if w is not None:
                deps.add(w)
            r = readers.get(k)
            if r:
                deps.update(r.values())
        for k in reads:
            r = readers.get(k)
            if r is None:
                r = readers[k] = {}
            r[eng if chan is None else ("dma", i)] = i
        for k in writes:
            lastw[k] = i
            readers[k] = {}
        deps.discard(i)
        if chan is not None:
            c = self.chan_count.get(chan, 0) + 1
            self.chan_count[chan] = c
            op.ordinal = c
        op.deps = tuple(deps)
        self.ops.append(op)
        return i


def build_slot_table():
    slots = []
    for l in range(2):
        slots.append(("cols", "w_qkv_a", l, 0, 512))
        slots.append(("cols", "w_qkv_a", l, 512, 512))
        slots.append(("kdup", "w_qkv_a", l, 1024, 256))
        slots.append(("cols", "w_qkv_a", l, 1280, 256))
        slots.append(("cols", "w_o_a", l, 0, 512))
        slots.append(("cols", "w_o_a", l, 512, 512))
        for s in range(8):
            slots.append(("cols", "w_up", l, s * 512, 512))
        for mh in range(2):
            for s in range(4):
                slots.append(("down", "w_down", l, mh, s))
    for j in range(2):
        slots.append(("cols", "w_q_b", j, 0, 512))
        slots.append(("cols", "w_q_b", j, 512, 512))
        if j == 0:
            slots.append(("kdup", "w_kv", None, 0, 256))
            slots.append(("cols", "w_kv", None, 256, 256))
        slots.append(("cols", "w_o_b", j, 0, 512))
        slots.append(("cols", "w_o_b", j, 512, 512))
        for s in range(8):
            slots.append(("cols", "w_up", 2 + j, s * 512, 512))
        for mh in range(2):
            for s in range(4):
                slots.append(("down", "w_down", 2 + j, mh, s))
    return slots


SLOTS = build_slot_table()
NSLOT = len(SLOTS)
PREP_GROUP_BOUNDS = [6, 22, 44, 66, NSLOT]


def prep_group(slot):
    for gi, b in enumerate(PREP_GROUP_BOUNDS):
        if slot < b:
            return gi
    return len(PREP_GROUP_BOUNDS) - 1


def build_program():
    nc = bass.Bass("TRN2", target_bir_lowering=False)

    def din(name, shape, dt=F32):
        return nc.dram_tensor(name, list(shape), dt, kind="ExternalInput").ap()

    def dout(name, shape, dt=F32):
        return nc.dram_tensor(name, list(shape), dt, kind="ExternalOutput").ap()

    xTp = din("xTp", [D, SEQ])
    xTs = din("xTs", [D, TS])
    ckTa = din("ckTa", [2, NSTREAM, 128, 4, 128])
    cva = din("cva", [2, NSTREAM, 128, 256])
    ckTb = din("ckTb", [NSTREAM, 128, 4, 512])
    cvb = din("cvb", [NSTREAM, 4, 128, 256])
    W = {
        "w_qkv_a": din("w_qkv_a", [2, D, 1536]),
        "w_o_a": din("w_o_a", [2, D, D]),
        "w_kv": din("w_kv", [D, 512]),
        "w_q_b": din("w_q_b", [2, D, D]),
        "w_o_b": din("w_o_b", [2, D, D]),
        "w_up": din("w_up", [4, D, 4096]),
        "w_down": din("w_down", [4, 4096, D]),
    }
    gvec_d = din("gvec", [128, 9 * 8])
    gqk_d = din("gqk", [128, 7])
    alibi_d = din("alibi", [128, 3 * 1024])
    biasB_d = din("biasB", [2, 128, 3 * 1024])
    biasBc_d = din("biasBc", [2, 128, 1024])
    sink_d = din("sinktile", [128, 1024])

    yTp = dout("yTp", [D, SEQ])
    yTs = dout("yTs", [D, TS])
    okA_p = dout("okA_p", [2, 64, 4, 128])
    ovA_p = dout("ovA_p", [2, 128, 256])
    okB_p = dout("okB_p", [64, 4, 512])
    ovB_p = dout("ovB_p", [512, 256])
    okA_s = dout("okA_s", [2, 64, 4, TS])
    ovA_s = dout("ovA_s", [2, TS, 256])
    okB_s = dout("okB_s", [64, 4, TS])
    ovB_s = dout("ovB_s", [TS, 256])

    scr = nc.dram_tensor("wscr", [NSLOT, 128, 4096], BF16, kind="Internal").ap()

    S = Sched()
    import contextlib
    es = contextlib.ExitStack()

    def sb(name, shape, dt):
        return es.enter_context(nc.sbuf_tensor("sb_" + name, list(shape), dt))

    xT = sb("xT", [128, 8, T], F32)
    xn = sb("xn", [128, 8, T], BF16)
    big = sb("big", [128, 32, T], BF16)
    qf = sb("qf", [128, 3, T], F32)
    sqr = sb("sqr", [128, 5, T], BF16)
    rs = sb("rs", [128, 2, T], F32)
    rl = sb("rl", [128, 2, T], F32)
    sc = sb("sc", [128, 3, 256], F32)
    rc = sb("rc", [128, 2, T], F32)
    yst = sb("yst", [128, 2, T], F32)
    vst = sb("vst", [128, 2, 256], F32)
    kTA = [sb("kTA%d" % l, [128, 4, 128 + T], BF16) for l in range(2)]
    vA = [sb("vA%d" % l, [128, 5, 576], BF16) for l in range(2)]
    kTB = sb("kTB", [128, 4, 512 + T], BF16)
    vB = sb("vB", [128, 8, 576], BF16)
    biasB = sb("biasB", [128, 2, 3, 4, 2, 2, 64], F32)
    alibi = sb("alibi", [128, 3, 4, 2, 2, 64], F32)
    esink = sb("esink", [128, 2, 4, 2, 64], F32)
    pT = sb("pT", [128, 8, 2, 128], BF16)
    wsl = sb("wsl", [128, NW, 8, 512], BF16)
    ones_bf = sb("ones_bf", [128, 128], BF16)
    blk_bf = sb("blk_bf", [128, 128], BF16)
    selE = sb("selE", [1, 128], BF16)
    selO = sb("selO", [1, 128], BF16)
    gvec = sb("gvec", [128, 9, 8], F32)
    gqk = sb("gqk", [128, 7], F32)
    PS = es.enter_context(nc.psum_tensor("PS", [128, 8, 4, 128], F32))

    def psk(bank):
        return [("psb", bank)]

    def bigk(i):
        return ("big", i)

    def bigk_all(i):
        if 8 <= i < 16:
            return [("big", i), ("oT", i - 8, 0), ("oT", i - 8, 1)]
        return [("big", i)]

    mm_ring = [4, 5, 6, 7, 0, 1, 2]
    NRING = 7
    STATS_BANK = 3
    lagq = []

    lag_keep = [0]

    def run_lagged(keep=0):
        n_ = max(0, len(lagq) - keep)
        q_ = lagq[:n_]
        del lagq[:n_]
        for f_ in q_:
            f_()
    ring_state = {"mm": 0, "qf": 0, "rs": 0, "rl": 0, "sc": 0, "rc": 0, "yst": 0, "kst": 0,
                  "vst": 0, "sq2": 0, "sqr": 0, "S": 0, "pT": 0, "pv": 0}

    def nxt(name, n):
        v = ring_state[name]
        ring_state[name] = v + 1
        return v % n

    def ps_bank_ap(bank, rows=128, n=T):
        return PS[0:rows, bank].rearrange("p a b -> p (a b)")[:, 0:n]

    CONST = "const"
    S.add("sp", lambda e: e.dma_start(out=gvec[:].rearrange("p a b -> p (a b)"), in_=gvec_d[:, :]),
          writes=["gvec"], chan=CONST, bulk=True)
    S.add("sp", lambda e: e.dma_start(out=gqk[:], in_=gqk_d[:, :]), writes=["gqk"], chan=CONST, bulk=True)
    S.add("sp", lambda e: e.dma_start(out=alibi[:].rearrange("p t g e j q -> p (t g e j q)"), in_=alibi_d[:, :]),
          writes=["alibi"], chan=CONST, bulk=True)
    for l in range(2):
        S.add("sp", lambda e, l=l: e.dma_start(
            out=biasB[:, l].rearrange("p t g e j q -> p (t g e j q)"), in_=biasB_d[l]),
            writes=[("biasB", l)], chan=CONST, bulk=True)

    S.add("dve", lambda e: e.memset(ones_bf[:], 1.0), writes=["ones"])
    S.add("dve", lambda e: e.memset(blk_bf[:], 0.0), writes=["blk"])
    S.add("dve", lambda e: e.memset(blk_bf[0:64, 0:64], 1.0), writes=["blk"])
    S.add("dve", lambda e: e.memset(blk_bf[64:128, 64:128], 1.0), writes=["blk"])
    S.add("dve", lambda e: e.memset(selE[:, 0:64], 0.0), writes=["sel"])
    S.add("dve", lambda e: e.memset(selE[:, 64:128], 1.0), writes=["sel"])
    S.add("dve", lambda e: e.memset(selO[:, 0:64], 1.0), writes=["sel"])
    S.add("dve", lambda e: e.memset(selO[:, 64:128], 0.0), writes=["sel"])
    for l in range(2):
        S.add("dve", lambda e, l=l: e.memset(vA[l][:].rearrange("p a b -> p (a b)"), 1.0),
              writes=[("vA", l, b) for b in range(5)])
    S.add("dve", lambda e: e.memset(vB[:].rearrange("p a b -> p (a b)"), 1.0),
          writes=[("vB", b) for b in range(8)])
    S.add("dve", lambda e: e.tensor_scalar(out=gqk[:, 0:2], in0=gqk[:, 0:2], scalar1=0.125, scalar2=None,
                                           op0=ALU.mult), reads=["gqk"], writes=["gqk"])
    S.add("dve", lambda e: e.tensor_scalar(out=gqk[:, 5:7], in0=gqk[:, 5:7], scalar1=0.125, scalar2=None,
                                           op0=ALU.mult), reads=["gqk"], writes=["gqk"])
    rl_flat = rl[:].rearrange("p a b -> p (a b)")
    S.add("sp", lambda e: e.dma_start(out=rl_flat, in_=sink_d[:, :]), writes=[("rl", 0), ("rl", 1)], chan="bc")
    S.add("act", lambda e: e.activation(out=esink[:].rearrange("p l g j q -> p (l g j q)"), in_=rl_flat, func=AF.Exp),
          reads=[("rl", 0), ("rl", 1)], writes=["esink"])
    for l in range(2):
        S.add("sp", lambda e, l=l: e.dma_start(out=rl_flat, in_=biasBc_d[l]),
              writes=[("rl", 0), ("rl", 1)], chan="bc")
        for t in range(3):
            S.add("dve", lambda e, l=l, t=t: e.tensor_tensor(
                out=biasB[:, l, t].rearrange("p g e j q -> p (g e j q)"),
                in0=biasB[:, l, t].rearrange("p g e j q -> p (g e j q)"),
                in1=rl_flat, op=ALU.subtract),
                reads=[("rl", 0), ("rl", 1), ("biasB", l)], writes=[("biasB", l)])

    for si, sd in enumerate(SLOTS):
        ch = ("prep", prep_group(si))
        kind, wname, l = sd[0], sd[1], sd[2]
        wap = W[wname] if l is None else W[wname][l]
        if kind == "cols":
            c0, w = sd[3], sd[4]
            src = wap.rearrange("(kc p) n -> p kc n", p=128)[:, :, c0:c0 + w]
            dst = scr[si].rearrange("p (kc n) -> p kc n", kc=8)[:, :, 0:w]
            S.add("pool", lambda e, src=src, dst=dst: e.dma_start(out=dst, in_=src),
                  writes=[("scr", si)], chan=ch, bulk=True)
        elif kind == "down":
            mh, s = sd[3], sd[4]
            src = wap.rearrange("(s kc p) n -> s p kc n", kc=8, p=128)[s][:, :, mh * 512:(mh + 1) * 512]
            dst = scr[si].rearrange("p (kc n) -> p kc n", kc=8)
            S.add("pool", lambda e, src=src, dst=dst: e.dma_start(out=dst, in_=src),
                  writes=[("scr", si)], chan=ch, bulk=True)
        else:
            c0 = sd[3]
            srcv = wap.rearrange("(kc p) n -> p kc n", p=128)
            dstv = scr[si].rearrange("p (kc g two d) -> p kc g two d", kc=8, g=4, two=2)
            for kc in range(8):
                for dup in range(2):
                    src = srcv[:, kc, c0:c0 + 256].rearrange("p (g d) -> p g d", g=4)
                    dst = dstv[:, kc, :, dup, :]
                    S.add("pool", lambda e, src=src, dst=dst: e.dma_start(out=dst, in_=src),
                          writes=[("scr", si, kc, dup)], chan=ch, bulk=True)

    wstate = {"issued": 0, "used": 0}
    n_tiles_total = NT_RUN + (1 if DO_SAMPLE else 0)
    total_uses = n_tiles_total * NSLOT

    def issue_weight_load(u):
        slot = u % NSLOT
        buf = u % NW
        sd = SLOTS[slot]
        w = sd[4] if sd[0] == "cols" else 512
        if sd[0] == "kdup":
            w = 512
        src = scr[slot].rearrange("p (kc n) -> p kc n", kc=8)[:, :, 0:w]
        dst = wsl[:, buf, :, 0:w]
        rk = [("scr", slot)]
        if sd[0] == "kdup":
            rk = [("scr", slot, kc, dup) for kc in range(8) for dup in range(2)]
        S.add("sp", lambda e, src=src, dst=dst: e.dma_start(out=dst, in_=src),
              reads=rk, writes=[("w", buf)], chan=("w", buf))

    def next_weights(expect_kind=None):
        u = wstate["used"]
        while wstate["issued"] < min(u + NW, total_uses):
            issue_weight_load(wstate["issued"])
            wstate["issued"] += 1
        wstate["used"] = u + 1
        assert expect_kind is None or SLOTS[u % NSLOT][0] == expect_kind
        return u % NW

    cur = {"rs": 0}
    qk_pool = [False]

    def norm_chunk(kc, N, use_pool, nidx, immediate=False):
        S.add("dve", lambda e: e.tensor_scalar(out=xn[:, kc, 0:N], in0=xT[:, kc, 0:N],
                                               scalar1=gvec[:, nidx, kc:kc + 1], scalar2=None, op0=ALU.mult),
              reads=[("x", kc), "gvec"], writes=[("xn", kc)])
        i = nxt("sqr", 5)
        eng = "pool" if use_pool else "dve"
        S.add(eng, lambda e: e.tensor_tensor(out=sqr[:, i, 0:N], in0=xT[:, kc, 0:N], in1=xT[:, kc, 0:N], op=ALU.mult),
              reads=[("x", kc)], writes=[("sqr", i)])

        def mm():
            S.add("pe", lambda e: e.matmul(ps_bank_ap(STATS_BANK, 128, N), ones_bf[:, :], sqr[:, i, 0:N],
                                           start=(kc == 0), stop=(kc == 7)),
                  reads=[("sqr", i), "ones"], excl=psk(STATS_BANK))
        if immediate:
            mm()
        else:
            lagq.append(mm)

    def rmsnorm(N):
        run_lagged()
        r = nxt("rs", 2)
        cur["rs"] = r
        S.add("act", lambda e: e.activation(out=rs[:, r, 0:N], in_=ps_bank_ap(STATS_BANK, 128, N), func=AF.Ln,
                                            scale=1.0 / D, bias=EPS),
              excl=psk(STATS_BANK), writes=[("rs", r)])
        S.add("act", lambda e: e.activation(out=rs[:, r, 0:N], in_=rs[:, r, 0:N], func=AF.Exp, scale=-0.5),
              reads=[("rs", r)], writes=[("rs", r)])

    def rescale_xg(nidx, N):
        for kc in range(8):
            S.add("dve", lambda e, kc=kc: e.tensor_scalar(out=xn[:, kc, 0:N], in0=xT[:, kc, 0:N],
                                                          scalar1=gvec[:, nidx, kc:kc + 1], scalar2=None, op0=ALU.mult),
                  reads=[("x", kc), "gvec"], writes=[("xn", kc)])

    def make_xv(N):
        r = cur["rs"]
        for kc in range(8):
            S.add("dve", lambda e, kc=kc: e.tensor_tensor(out=big[:, 16 + kc, 0:N], in0=xn[:, kc, 0:N],
                                                          in1=rs[:, r, 0:N], op=ALU.mult),
                  reads=[("xn", kc), ("rs", r)], writes=[bigk(16 + kc)])

    def qk_norm(bank, N, gcol, dst_ap, dst_keys, kout=None, fold=True):
        q = nxt("qf", 3)
        r0_ = cur["rs"]
        if fold:
            S.add("dve", lambda e: e.tensor_tensor(out=qf[:, q, 0:N], in0=ps_bank_ap(bank, 128, N), in1=rs[:, r0_, 0:N],
                                                   op=ALU.mult),
                  reads=[("rs", r0_)], excl=psk(bank), writes=[("qf", q)])
        else:
            S.add("act", lambda e: e.activation(out=qf[:, q, 0:N], in_=ps_bank_ap(bank, 128, N), func=AF.Copy),
                  excl=psk(bank), writes=[("qf", q)])
        s2 = 24 + nxt("sq2", 4)
        S.add("pool" if qk_pool[0] else "dve",
              lambda e: e.tensor_tensor(out=big[:, s2, 0:N], in0=qf[:, q, 0:N], in1=qf[:, q, 0:N], op=ALU.mult),
              reads=[("qf", q)], writes=[bigk(s2)])
        def stage2():
            b2 = mm_ring[nxt("mm", NRING)]
            S.add("pe", lambda e: e.matmul(ps_bank_ap(b2, 128, N), blk_bf[:, :], big[:, s2, 0:N], start=True, stop=True),
                  reads=[bigk(s2), "blk"], excl=psk(b2))
            r = nxt("rl", 2)
            S.add("act", lambda e: e.activation(out=rl[:, r, 0:N], in_=ps_bank_ap(b2, 128, N), func=AF.Ln,
                                                scale=1.0 / 64, bias=EPS),
                  excl=psk(b2), writes=[("rl", r)])
            S.add("act", lambda e: e.activation(out=rl[:, r, 0:N], in_=rl[:, r, 0:N], func=AF.Exp, scale=-0.5),
                  reads=[("rl", r)], writes=[("rl", r)])
            S.add("dve", lambda e: e.scalar_tensor_tensor(
                out=dst_ap, in0=qf[:, q, 0:N], scalar=gqk[:, gcol:gcol + 1], in1=rl[:, r, 0:N],
                op0=ALU.mult, op1=ALU.mult),
                reads=[("qf", q), ("rl", r), "gqk"], writes=dst_keys)
            if kout is not None:
                out_ap, c0, c1 = kout
                k = nxt("yst", 2)
                S.add("dve", lambda e: e.scalar_tensor_tensor(
                    out=yst[:, k, 0:N], in0=qf[:, q, 0:N], scalar=gqk[:, gcol:gcol + 1], in1=rl[:, r, 0:N],
                    op0=ALU.mult, op1=ALU.mult),
                    reads=[("qf", q), ("rl", r), "gqk"], writes=[("yst", k)])
                S.add("sp", lambda e: e.dma_start(out=out_ap, in_=yst[0:64, k, c0:c1]),
                      reads=[("yst", k)], chan=("yst", k))
        lag_keep[0] = 1
        lagq.append(stage2)

    def proj_fm(wbuf, mcol, rhs_fn, N, nk=8):
        bank = mm_ring[nxt("mm", NRING)]
        for kc in range(nk):
            S.add("pe", lambda e, kc=kc: e.matmul(ps_bank_ap(bank, 128, N), wsl[:, wbuf, kc, mcol:mcol + 128],
                                                  rhs_fn(kc)[0], start=(kc == 0), stop=(kc == nk - 1)),
                  reads=[("w", wbuf)] + rhs_fn(kc)[1], excl=psk(bank))
        run_lagged(lag_keep[0])
        return bank

    def proj_group_kc_outer(wbuf, rhs_fn, N, nk=8):
        import os
        if os.environ.get("DBG_NO_KCO"):
            return [proj_fm(wbuf, mm * 128, rhs_fn, N, nk) for mm in range(4)]
        banks = [mm_ring[nxt("mm", NRING)] for _ in range(4)]
        for kc in range(nk):
            for mm in range(4):
                S.add("pe", lambda e, kc=kc, mm=mm: e.matmul(ps_bank_ap(banks[mm], 128, N),
                                                             wsl[:, wbuf, kc, mm * 128:(mm + 1) * 128],
                                                             rhs_fn(kc)[0], start=(kc == 0), stop=(kc == nk - 1)),
                      reads=[("w", wbuf)] + rhs_fn(kc)[1], excl=psk(banks[mm]))
        run_lagged()
        return banks

    def xn_rhs(N):
        return lambda kc: (xn[:, kc, 0:N], [("xn", kc)])

    def v_tok(wbuf, N, vbuf, vkey, blk0, sample, vout):
        lag_keep[0] = 0
        run_lagged()
        if not sample:
            groups = [(tb * 128, 128, blk0 + tb) for tb in range(N // 128)]
        else:
            groups = [(s * DEC, DEC, blk0 + s) for s in range(NSTREAM)]
        for gi, (t0, M, blk) in enumerate(groups):
            bank = mm_ring[nxt("mm", NRING)]
            for kc in range(8):
                S.add("pe", lambda e, kc=kc, t0=t0, M=M, bank=bank: e.matmul(
                    ps_bank_ap(bank, M, 256), big[:, 16 + kc, t0:t0 + M], wsl[:, wbuf, kc, 0:256],
                    start=(kc == 0), stop=(kc == 7)),
                    reads=[("w", wbuf), bigk(16 + kc)], excl=psk(bank))
            dstv = vbuf[0:M, blk, 64:576].rearrange("p (g c) -> p g c", g=4)[:, :, 0:64]
            S.add("act", lambda e, dstv=dstv, bank=bank, M=M: e.activation(
                out=dstv, in_=ps_bank_ap(bank, M, 256).rearrange("p (g d) -> p g d", g=4), func=AF.Copy),
                excl=psk(bank), writes=[vkey(blk)])
            if vout is not None:
                o = vout(gi, t0, M)
                if o is not None:
                    k = nxt("vst", 2)
                    S.add("dve", lambda e, k=k, bank=bank, M=M: e.tensor_copy(out=vst[0:M, k, :],
                                                                                in_=ps_bank_ap(bank, M, 256)),
                          excl=psk(bank), writes=[("vst", k)])
                    S.add("sp", lambda e, k=k, o=o, M=M: e.dma_start(out=o, in_=vst[0:M, k, :]),
                          reads=[("vst", k)], chan=("vst", k))

    def residual_add(bank, m, N, to_out=None, nxt_norm=None, nidx=None):
        if to_out is None:
            S.add("dve", lambda e: e.tensor_tensor(out=xT[:, m, 0:N], in0=ps_bank_ap(bank, 128, N),
                                                   in1=xT[:, m, 0:N], op=ALU.add),
                  excl=psk(bank), reads=[("x", m)], writes=[("x", m)])
            if nidx is not None:
                norm_chunk(m, N, nxt_norm, nidx)
        else:
            y = nxt("yst", 2)
            S.add("dve", lambda e: e.tensor_tensor(out=yst[:, y, 0:N], in0=ps_bank_ap(bank, 128, N),
                                                   in1=xT[:, m, 0:N], op=ALU.add),
                  excl=psk(bank), reads=[("x", m)], writes=[("yst", y)])
            S.add("sp", lambda e: e.dma_start(out=to_out(m), in_=yst[:, y, 0:N]),
                  reads=[("yst", y)], chan=("yst", y))

    def wo_proj(N, nxt_norm, nidx):
        for half in range(2):
            wb = next_weights("cols")
            for mm in range(4):
                m = half * 4 + mm
                bank = proj_fm(wb, mm * 128, lambda kc: (big[:, 8 + kc, 0:N], bigk_all(8 + kc)), N)
                residual_add(bank, m, N, None, nxt_norm, nidx)

    def relu2_evac(bank, m, N):
        r = nxt("rl", 2)
        r0_ = cur["rs"]
        S.add("dve", lambda e: e.scalar_tensor_tensor(out=rl[:, r, 0:N], in0=ps_bank_ap(bank, 128, N), scalar=0.0,
                                                      in1=rs[:, r0_, 0:N], op0=ALU.max, op1=ALU.mult),
              reads=[("rs", r0_)], excl=psk(bank), writes=[("rl", r)])
        S.add("act", lambda e: e.activation(out=big[:, m, 0:N], in_=rl[:, r, 0:N], func=AF.Square),
              reads=[("rl", r)], writes=bigk_all(m))

    def mlp(N, to_out=None, nxt_norm=None, nidx=None, after_chunk=None):
        for s in range(8):
            wb = next_weights("cols")
            for mm in range(4):
                m = s * 4 + mm
                bank = proj_fm(wb, mm * 128, xn_rhs(N), N)
                relu2_evac(bank, m, N)
        for mh in range(2):
            for s in range(4):
                wb = next_weights("down")
                for mm in range(4):
                    for kc in range(8):
                        S.add("pe", lambda e, mm=mm, kc=kc, s=s, wb=wb: e.matmul(
                            ps_bank_ap(4 + mm, 128, N), wsl[:, wb, kc, mm * 128:(mm + 1) * 128],
                            big[:, s * 8 + kc, 0:N], start=(s == 0 and kc == 0), stop=(s == 3 and kc == 7)),
                            reads=[("w", wb)] + bigk_all(s * 8 + kc), excl=psk(4 + mm))
                run_lagged()
            for mm in range(4):
                residual_add(4 + mm, mh * 4 + mm, N, to_out, nxt_norm, nidx)
                if after_chunk is not None:
                    after_chunk(mh * 4 + mm)

    def pieces_for(cc, L, first_valid_blk, lo_rows):
        out = []
        if cc % 2 == 0:
            nfull = L // 2
            b0 = (cc - L) // 2
            for k in range(nfull):
                t = 1 if k == nfull - 1 else None
                out.append((b0 + k, 128, [(0, 128, t)]))
            out.append((cc // 2, lo_rows, [(0, lo_rows, 0)]))
        else:
            nfull = L // 2
            bh = (cc - L - 1) // 2
            out.append((bh, 128, [(64, 128, 0 if L == 2 else None)]))
            b0 = (cc - L + 1) // 2
            for k in range(nfull):
                if k == nfull - 1:
                    out.append((b0 + k, 128, [(0, 128, 2)]))
                elif k == nfull - 2:
                    out.append((b0 + k, 128, [(0, 64, None), (64, 128, 0)]))
                else:
                    out.append((b0 + k, 128, [(0, 128, None)]))
        return [p for p in out if p[0] >= first_valid_blk]

    def attention(iters, is_a, lidx):
        hooks = {"begin_now": [], "begin_next": [], "end_now": [], "end_next": []}
        flat = []
        for ii, it in enumerate(iters):
            it["slots"] = [None] * len(it["pieces"])
            for pi in range(len(it["pieces"])):
                flat.append((ii, pi))
        groups = [flat[k:k + 4] for k in range(0, len(flat), 4)]
        last_group_of_iter = {}
        for gi, grp in enumerate(groups):
            for (ii, pi) in grp:
                last_group_of_iter[ii] = gi

        def qk_group(gi):
            sset = nxt("S", 2)
            for qi, (ii, pi) in enumerate(groups[gi]):
                it = iters[ii]
                g, c0, nq = it["g"], it["c0"], it["nq"]
                blk, M, subs = it["pieces"][pi]
                ptl = nxt("pT", 8)
                it["slots"][pi] = (sset, qi, ptl)
                kc0 = it["keycol"](blk)
                for e_ in range(2):
                    bank = e_ * 2 + sset
                    S.add("pe", lambda e, e_=e_, bank=bank, qi=qi, kc0=kc0, M=M, it=it, g=g, c0=c0, nq=nq: e.matmul(
                        PS[0:M, bank, qi, 0:2 * nq],
                        it["kT"][e_ * 64:(e_ + 1) * 64, g, kc0:kc0 + M],
                        big[e_ * 64:(e_ + 1) * 64, 2 * g:2 * g + 2, c0:c0 + nq], start=True, stop=True),
                        reads=[it["kTkey"](blk, g), bigk(2 * g), bigk(2 * g + 1)], excl=psk(bank))

        def exp_group(gi):
            grp = groups[gi]
            k = 0
            while k < len(grp):
                ii, pi = grp[k]
                it = iters[ii]
                g, c0, nq = it["g"], it["c0"], it["nq"]
                blk, M, subs = it["pieces"][pi]
                sset, qi, ptl = it["slots"][pi]
                sbanks = psk(sset) + psk(2 + sset)
                if subs == [(0, 128, None)]:
                    n_ = 1
                    while k + n_ < len(grp) and not DBG_NOMERGE:
                        ii2, pi2 = grp[k + n_]
                        it2 = iters[ii2]
                        if it2["pieces"][pi2][2] != [(0, 128, None)] or it2["nq"] != nq:
                            break
                        if it2["slots"][pi2][2] != ptl + n_:
                            break
                        n_ += 1
                    sview = PS[0:128, sset:4:2, qi:qi + n_, 0:2 * nq]
                    oview = pT[0:128, ptl:ptl + n_, :, 0:2 * nq].rearrange("p s e c -> p e s c")
                    S.add("act", lambda e, sview=sview, oview=oview: e.activation(out=oview, in_=sview, func=AF.Exp),
                          excl=sbanks, writes=[("pT", ptl + t_) for t_ in range(n_)])
                    k += n_
                    continue
                for (r0, r1, bt) in subs:
                    sview = PS[r0:r1, sset:4:2, qi, 0:2 * nq]
                    oview = pT[r0:r1, ptl, :, 0:2 * nq]
                    if bt is None:
                        S.add("act", lambda e, sview=sview, oview=oview: e.activation(out=oview, in_=sview, func=AF.Exp),
                              excl=sbanks, writes=[("pT", ptl)])
                        continue
                    s_ = nxt("sc", 3)
                    scv = sc[r0:r1, s_, 0:4 * nq].rearrange("p (e n) -> p e n", e=2)
                    btile = alibi if is_a else biasB[:, lidx]
                    bkey = "alibi" if is_a else ("biasB", lidx)
                    S.add("dve", lambda e, sview=sview, scv=scv, r0=r0, r1=r1, bt=bt, g=g, nq=nq, btile=btile: e.tensor_tensor(
                        out=scv.rearrange("p e (j q) -> p e j q", j=2),
                        in0=sview.rearrange("p e (j q) -> p e j q", j=2),
                        in1=btile[r0:r1, bt, g, :, :, 0:nq], op=ALU.add),
                        reads=[bkey], excl=sbanks, writes=[("sc", s_)])
                    S.add("act", lambda e, scv=scv, oview=oview: e.activation(out=oview, in_=scv, func=AF.Exp),
                          reads=[("sc", s_)], writes=[("pT", ptl)])
                k += 1

        def pv_phase(it):
            g, c0, nq = it["g"], it["c0"], it["nq"]
            idx_ = pvc[0]
            pvc[0] += 1
            pvs, qd = (idx_ // 4) % 2, idx_ % 4
            bE, bO = 4 + 2 * pvs, 5 + 2 * pvs
            npieces = len(it["pieces"])
            for pi, (blk, M, subs) in enumerate(it["pieces"]):
                sset, qi, ptl = it["slots"][pi]
                r0 = min(s_[0] for s_ in subs)
                r1 = max(s_[1] for s_ in subs)
                vb = it["vblk"](blk)
                for e_ in range(2):
                    vc0 = 64 + 128 * g if e_ == 0 else 128 * g
                    last = (pi == npieces - 1)
                    bank = bE if e_ == 0 else bO
                    S.add("pe", lambda e, e_=e_, r0=r0, r1=r1, vb=vb, vc0=vc0, ptl=ptl, pi=pi, last=last, it=it, nq=nq, bank=bank:
                          e.matmul(PS[0:128, bank, qd, 0:2 * nq], it["vbuf"][r0:r1, vb, vc0:vc0 + 128],
                                   pT[r0:r1, ptl, e_, 0:2 * nq], start=(pi == 0), stop=last),
                          reads=[it["vkey"](vb), ("pT", ptl)], excl=psk(bank))
            it["pvb"] = (bE, bO, qd)

        def pv_norm(its):
            nq = its[0]["nq"]
            n_ = len(its)
            bE, bO, _ = its[0]["pvb"]
            assert all(t_["pvb"][0] == bE and t_["pvb"][2] == i_ for i_, t_ in enumerate(its))
            r = nxt("rc", 2)
            W_ = 2 * nq
            if is_a:
                g0 = its[0]["g"]
                assert all(t_["g"] == g0 + i_ for i_, t_ in enumerate(its))
                S.add("dve", lambda e: e.tensor_tensor(
                    out=rc[64:128, r, 0:n_ * W_].rearrange("p (i j q) -> p i j q", i=n_, j=2),
                    in0=PS[64:128, bE, 0:n_, 0:W_].rearrange("p i (j q) -> p i j q", j=2),
                    in1=esink[64:128, lidx, g0:g0 + n_, :, 0:nq], op=ALU.add),
                    reads=["esink"], excl=psk(bE), writes=[("rc", r, "E")])
                S.add("dve", lambda e: e.tensor_tensor(
                    out=rc[0:64, r, 0:n_ * W_].rearrange("p (i j q) -> p i j q", i=n_, j=2),
                    in0=PS[0:64, bO, 0:n_, 0:W_].rearrange("p i (j q) -> p i j q", j=2),
                    in1=esink[0:64, lidx, g0:g0 + n_, :, 0:nq], op=ALU.add),
                    reads=["esink"], excl=psk(bO), writes=[("rc", r, "O")])
                S.add("act", lambda e: e.activation(out=rc[:, r, 0:n_ * W_], in_=rc[:, r, 0:n_ * W_], func=AF.Ln),
                      reads=[("rc", r, "E"), ("rc", r, "O")], writes=[("rc", r, "E"), ("rc", r, "O")])
            else:
                S.add("act", lambda e: e.activation(
                    out=rc[64:128, r, 0:n_ * W_].rearrange("p (i c) -> p i c", i=n_),
                    in_=PS[64:128, bE, 0:n_, 0:W_], func=AF.Ln),
                    excl=psk(bE), writes=[("rc", r, "E")])
                S.add("act", lambda e: e.activation(
                    out=rc[0:64, r, 0:n_ * W_].rearrange("p (i c) -> p i c", i=n_),
                    in_=PS[0:64, bO, 0:n_, 0:W_], func=AF.Ln),
                    excl=psk(bO), writes=[("rc", r, "O")])
            def stage_b():
                S.add("act", lambda e: e.activation(out=rc[:, r, 0:n_ * W_], in_=rc[:, r, 0:n_ * W_],
                                                    func=AF.Exp, scale=-1.0),
                      reads=[("rc", r, "E"), ("rc", r, "O")], writes=[("rc", r, "E"), ("rc", r, "O")])

            def stage_c():
                mults()
            hooks["begin_next"].append(stage_b)
            hooks["end_next"].append(stage_c)

            def mults():
              g0_ = its[0]["g"]
              aligned = all(t_["g"] == g0_ + i_ and t_["c0"] == its[0]["c0"] for i_, t_ in enumerate(its))
              if aligned:
                  c0 = its[0]["c0"]
                  okE = [("oT", 2 * (g0_ + i_) + j_, 0) for i_ in range(n_) for j_ in range(2)]
                  okO = [("oT", 2 * (g0_ + i_) + j_, 1) for i_ in range(n_) for j_ in range(2)]
                  S.add("dve", lambda e: e.tensor_tensor(
                      out=big[0:64, 8 + 2 * g0_:8 + 2 * g0_ + 2 * n_, c0:c0 + nq].rearrange("p (g j) q -> p g j q", j=2),
                      in0=PS[0:64, bE, 0:n_, 0:W_].rearrange("p g (j q) -> p g j q", j=2),
                      in1=rc[64:128, r, 0:n_ * W_].rearrange("p (g j q) -> p g j q", g=n_, j=2), op=ALU.mult),
                      reads=[("rc", r, "E")], excl=psk(bE), writes=okE)
                  S.add("dve", lambda e: e.tensor_tensor(
                      out=big[64:128, 8 + 2 * g0_:8 + 2 * g0_ + 2 * n_, c0:c0 + nq].rearrange("p (g j) q -> p g j q", j=2),
                      in0=PS[64:128, bO, 0:n_, 0:W_].rearrange("p g (j q) -> p g j q", j=2),
                      in1=rc[0:64, r, 0:n_ * W_].rearrange("p (g j q) -> p g j q", g=n_, j=2), op=ALU.mult),
                      reads=[("rc", r, "O")], excl=psk(bO), writes=okO)
                  return
              for i_, it in enumerate(its):
                  g, c0 = it["g"], it["c0"]
                  S.add("dve", lambda e, i_=i_, g=g, c0=c0: e.tensor_tensor(
                      out=big[0:64, 8 + 2 * g:8 + 2 * g + 2, c0:c0 + nq],
                      in0=PS[0:64, bE, i_, 0:W_].rearrange("p (j q) -> p j q", j=2),
                      in1=rc[64:128, r, i_ * W_:(i_ + 1) * W_].rearrange("p (j q) -> p j q", j=2), op=ALU.mult),
                      reads=[("rc", r, "E")], excl=psk(bE), writes=[("oT", 2 * g, 0), ("oT", 2 * g + 1, 0)])
                  S.add("dve", lambda e, i_=i_, g=g, c0=c0: e.tensor_tensor(
                      out=big[64:128, 8 + 2 * g:8 + 2 * g + 2, c0:c0 + nq],
                      in0=PS[64:128, bO, i_, 0:W_].rearrange("p (j q) -> p j q", j=2),
                      in1=rc[0:64, r, i_ * W_:(i_ + 1) * W_].rearrange("p (j q) -> p j q", j=2), op=ALU.mult),
                      reads=[("rc", r, "O")], excl=psk(bO), writes=[("oT", 2 * g, 1), ("oT", 2 * g + 1, 1)])

        lag_keep[0] = 0
        run_lagged()
        ng = len(groups)
        pvc = [0]
        batches = [[]]

        def run_list(name):
            q_ = hooks[name][:]
            del hooks[name][:]
            for f_ in q_:
                f_()

        def flush_hooks():
            run_list("begin_now")
            run_list("begin_next")
            run_list("end_now")
            run_list("end_next")

        for gi in range(ng + 1):
            hooks["begin_now"].extend(hooks["begin_next"])
            del hooks["begin_next"][:]
            run_list("begin_now")
            if gi < ng:
                qk_group(gi)
            if gi >= 1:
                exp_group(gi - 1)
                for ii in range(len(iters)):
                    if last_group_of_iter[ii] == gi - 1:
                        if len(batches[-1]) == 4:
                            flush_hooks()
                            batches.append([])
                        pv_phase(iters[ii])
                        batches[-1].append(iters[ii])
                        if len(batches) >= 2 and len(batches[-1]) == 1:
                            pv_norm(batches.pop(0))
            run_list("end_now")
            hooks["end_now"].extend(hooks["end_next"])
            del hooks["end_next"][:]
        flush_hooks()
        for bt_ in batches:
            if bt_:
                pv_norm(bt_)
        flush_hooks()

    preloaded = [False]

    def emit_x_load(ti, sample, kc):
        if sample:
            src = xTs.rearrange("(kc p) t -> p kc t", p=128)
            n_ = TS
        else:
            src = xTp.rearrange("(kc p) t -> p kc t", p=128)[:, :, ti * T:(ti + 1) * T]
            n_ = T
        S.add("sp", lambda e: e.dma_start(out=xT[:, kc, 0:n_], in_=src[:, kc, :]),
              writes=[("x", kc)], chan=("x", kc))

    def run_tile(ti, sample, nxt=None):
        N = TS if sample else T
        first = (ti == 0) and not sample
        last_prompt = (ti == NT - 1) and not sample
        nq = DEC if sample else 64
        if sample:
            src = xTs.rearrange("(kc p) t -> p kc t", p=128)
        else:
            src = xTp.rearrange("(kc p) t -> p kc t", p=128)[:, :, ti * T:(ti + 1) * T]
        if not preloaded[0]:
            for kc in range(8):
                emit_x_load(ti, sample, kc)
        preloaded[0] = False
        use_pool = sample or ti > 0
        qk_pool[0] = use_pool
        for kc in range(8):
            norm_chunk(kc, N, use_pool, 0, immediate=True)

        def make_iters(kind, l):
            its = []
            if kind == "A":
                kT, vbuf, L, hist_blocks = kTA[l], vA[l], 2, 1
                kTkey = lambda blk, g: ("kTA", l, "hist" if blk < 1 else "cur", g)
                vkey = lambda blk: ("vA", l, blk)
            else:
                kT, vbuf, L, hist_blocks = kTB, vB, 8, 4
                kTkey = lambda blk, g: ("kTB", "hist" if blk < 4 else "cur", g)
                vkey = lambda blk: ("vB", blk)
            if not sample:
                for c in range(8):
                    cc = c + 2 * hist_blocks
                    pcs = pieces_for(cc, L, hist_blocks if first else 0, 64)
                    for g in range(4):
                        its.append(dict(c0=c * 64, nq=64, g=g, pieces=pcs, kT=kT, kTkey=kTkey,
                                        keycol=lambda blk: blk * 128, vbuf=vbuf, vkey=vkey, vblk=lambda blk: blk))
            else:
                for s in range(NSTREAM):
                    cc = 2 * hist_blocks
                    pcs = pieces_for(cc, L, 0, DEC)
                    for g in range(4):
                        d = dict(c0=s * DEC, nq=DEC, g=g, pieces=pcs, kT=kT, kTkey=kTkey,
                                 keycol=(lambda blk, s=s, hb=hist_blocks: blk * 128 if blk < hb else hb * 128 + s * DEC),
                                 vbuf=vbuf, vkey=vkey, stream=s,
                                 vblk=(lambda blk, s=s, hb=hist_blocks: blk if blk < hb else hb + s))
                        its.append(d)
            return its

        def run_attention(kind, l):
            its = make_iters(kind, l)
            if not sample:
                attention(its, kind == "A", l)
            else:
                for s in range(NSTREAM):
                    load_cache(kind, l, s)
                    attention([d for d in its if d["stream"] == s], kind == "A", l)

        def load_cache(kind, l, s):
            if kind == "A":
                S.add("pool", lambda e: e.dma_start(out=kTA[l][:, :, 0:128], in_=ckTa[l, s]),
                      writes=[("kTA", l, "hist", g) for g in range(4)], chan=("ck", "A", l))
                dstv = vA[l][:, 0, 64:576].rearrange("p (g c) -> p g c", g=4)[:, :, 0:64]
                S.add("pool", lambda e: e.dma_start(out=dstv, in_=cva[l, s].rearrange("p (g d) -> p g d", g=4)),
                      writes=[("vA", l, 0)], chan=("cv", "A", l))
            else:
                S.add("pool", lambda e: e.dma_start(out=kTB[:, :, 0:512], in_=ckTb[s]),
                      writes=[("kTB", "hist", g) for g in range(4)], chan=("ck", "B"))
                for b in range(4):
                    dstv = vB[:, b, 64:576].rearrange("p (g c) -> p g c", g=4)[:, :, 0:64]
                    S.add("pool", lambda e, b=b, dstv=dstv: e.dma_start(
                        out=dstv, in_=cvb[s, b].rearrange("p (g d) -> p g d", g=4)),
                        writes=[("vB", b)], chan=("cv", "B", b))

        for l in range(2):
            rmsnorm(N)
            for half in range(2):
                wb = next_weights("cols")
                for mm in range(4):
                    j = half * 4 + mm
                    bank = proj_fm(wb, mm * 128, xn_rhs(N), N)
                    qk_norm(bank, N, l, big[:, j, 0:N], [bigk(j)])
            wb = next_weights("kdup")
            for g in range(4):
                bank = proj_fm(wb, g * 128, xn_rhs(N), N)
                kout = None
                if sample:
                    kout = (okA_s[l, :, g, :], 0, TS)
                elif last_prompt:
                    kout = (okA_p[l, :, g, :], T - 128, T)
                qk_norm(bank, N, 2 + l, kTA[l][:, g, 128:128 + N], [("kTA", l, "cur", g)], kout)
            wb = next_weights("cols")
            make_xv(N)
            if sample:
                vout = lambda gi, t0, M, l=l: ovA_s[l, t0:t0 + M, :]
            elif last_prompt:
                vout = lambda gi, t0, M, l=l: (ovA_p[l, :, :] if gi == 3 else None)
            else:
                vout = None
            v_tok(wb, N, vA[l], lambda blk, l=l: ("vA", l, blk), 1, sample, vout)
            run_attention("A", l)
            wo_proj(N, use_pool, 4 + l)
            rmsnorm(N)
            mlp(N, None, use_pool, 1 if l == 0 else 2)
            if not sample and not last_prompt:
                S.add("pool", lambda e, l=l: e.tensor_copy(out=kTA[l][:, :, 0:128], in_=kTA[l][:, :, T:T + 128]),
                      reads=[("kTA", l, "cur", g) for g in range(4)],
                      writes=[("kTA", l, "hist", g) for g in range(4)])
                S.add("pool", lambda e, l=l: e.tensor_copy(out=vA[l][:, 0, :], in_=vA[l][:, 4, :]),
                      reads=[("vA", l, 4)], writes=[("vA", l, 0)])
        rmsnorm(N)

        def q_proj_b(j2):
            for half in range(2):
                wb = next_weights("cols")
                for mm in range(4):
                    j = half * 4 + mm
                    bank = proj_fm(wb, mm * 128, xn_rhs(N), N)
                    qk_norm(bank, N, 5 + j2, big[:, j, 0:N], [bigk(j)])

        r8 = cur["rs"]
        for kc in range(8):
            S.add("dve", lambda e, kc=kc: e.scalar_tensor_tensor(
                out=big[:, 16 + kc, 0:N], in0=xT[:, kc, 0:N], scalar=gvec[:, 8, kc:kc + 1], in1=rs[:, r8, 0:N],
                op0=ALU.mult, op1=ALU.mult),
                reads=[("x", kc), ("rs", r8), "gvec"], writes=[bigk(16 + kc)])
        q_proj_b(0)
        xv_rhs = lambda kc: (big[:, 16 + kc, 0:N], [bigk(16 + kc)])
        wb = next_weights("kdup")
        for g in range(4):
            bank = proj_fm(wb, g * 128, xv_rhs, N)
            kout = None
            if sample:
                kout = (okB_s[:, g, :], 0, TS)
            elif last_prompt:
                kout = (okB_p[:, g, :], 0, T)
            qk_norm(bank, N, 4, kTB[:, g, 512:512 + N], [("kTB", "cur", g)], kout, fold=False)
        wb = next_weights("cols")
        if sample:
            vout = lambda gi, t0, M: ovB_s[t0:t0 + M, :]
        elif last_prompt:
            vout = lambda gi, t0, M: ovB_p[t0:t0 + M, :]
        else:
            vout = None
        v_tok(wb, N, vB, lambda blk: ("vB", blk), 4, sample, vout)
        for j2 in range(2):
            if j2 == 1:
                rmsnorm(N)
                q_proj_b(1)
            run_attention("B", j2)
            wo_proj(N, use_pool, 6 + j2)
            rmsnorm(N)
            if j2 == 1:
                if sample:
                    to_out = lambda m: yTs[m * 128:(m + 1) * 128, :]
                else:
                    to_out = lambda m, ti=ti: yTp[m * 128:(m + 1) * 128, ti * T:(ti + 1) * T]
                if nxt is not None:
                    mlp(N, to_out, None, None, after_chunk=lambda m: emit_x_load(nxt[0], nxt[1], m))
                    preloaded[0] = True
                else:
                    mlp(N, to_out, None, None)
            else:
                mlp(N, None, use_pool, 3)
        if not sample and not last_prompt:
            S.add("pool", lambda e: e.tensor_copy(out=kTB[:, :, 0:512], in_=kTB[:, :, 512:1024]),
                  reads=[("kTB", "cur", g) for g in range(4)], writes=[("kTB", "hist", g) for g in range(4)])
            S.add("pool", lambda e: e.tensor_copy(out=vB[:, 0:4, :], in_=vB[:, 4:8, :]),
                  reads=[("vB", b) for b in range(4, 8)], writes=[("vB", b) for b in range(4)])

    for ti in range(NT_RUN):
        if ti + 1 < NT_RUN:
            nxt_ = (ti + 1, False)
        elif DO_SAMPLE:
            nxt_ = (0, True)
        else:
            nxt_ = None
        run_tile(ti, False, nxt_)
        run_lagged()
    if DO_SAMPLE:
        run_tile(0, True, None)
        run_lagged()

    ops = S.ops
    ENGS = ["pe", "act", "dve", "pool", "sp"]
    def needs_wait(op, d):
        dop = ops[d]
        if dop.chan is None and op.chan is None and dop.eng == op.eng:
            if dop.eng == "pe":
                return False
            if op.raw is not None and d not in op.raw:
                return False
        return True

    for op in ops:
        for d in op.deps:
            dop = ops[d]
            if dop.chan is None and needs_wait(op, d):
                dop.target = True
    counters = {e: 0 for e in ENGS}
    for op in ops:
        if op.chan is None and op.target:
            counters[op.eng] += 1
            op.idx = counters[op.eng]

    sem_eng = {e: es.enter_context(nc.semaphore("s_" + e)) for e in ["pe", "act", "dve", "pool"]}
    chans = sorted(S.chan_count.keys(), key=str)
    sem_ch = {c: es.enter_context(nc.semaphore("c%d" % i)) for i, c in enumerate(chans)}
    chan_total = dict(S.chan_count)

    def token(dop):
        if dop.chan is None:
            return (("e", dop.eng), dop.idx)
        if dop.bulk:
            return (("c", dop.chan), 16 * chan_total[dop.chan])
        return (("c", dop.chan), 16 * dop.ordinal)

    def sem_of(key):
        return sem_eng[key[1]] if key[0] == "e" else sem_ch[key[1]]

    per_eng = {e: [] for e in ENGS}
    for op in ops:
        per_eng[op.eng].append(op)

    out_chans = [c for c in chans if isinstance(c, tuple) and c[0] in ("yst", "kst", "vst")]

    def emit(engname, eng):
        waited = {}
        for op in per_eng[engname]:
            need = {}
            for d in op.deps:
                dop = ops[d]
                if not needs_wait(op, d):
                    continue
                k, v = token(dop)
                if v > need.get(k, 0):
                    need[k] = v
            for k, v in need.items():
                if waited.get(k, 0) < v:
                    eng.wait_ge(sem_of(k), v)
                    waited[k] = v
            ins = op.fn(eng)
            if op.chan is not None:
                ins.then_inc(sem_ch[op.chan], 16)
            elif op.target:
                ins.then_inc(sem_eng[engname], 1)
        if engname == "sp":
            for c in out_chans:
                eng.wait_ge(sem_ch[c], 16 * chan_total[c])

    block = es.enter_context(nc.Block())

    @block.tensor
    def _(e):
        emit("pe", e)

    @block.scalar
    def _(e):
        emit("act", e)

    @block.vector
    def _(e):
        emit("dve", e)

    @block.gpsimd
    def _(e):
        emit("pool", e)

    @block.sync
    def _(e):
        emit("sp", e)

    es.close()
    return nc


def _bias_tables(rel_bias_b):
    p = np.arange(128)
    ki = (p % 64)[:, None, None]
    half = (p // 64)[:, None, None]
    jt = np.array([[0, 2], [2, 1], [1, 0]])
    t = np.arange(3)[None, :, None]
    j = jt[t, half]
    q = np.arange(64)[None, None, :]
    rel = 64 * j + q - ki
    idx = np.clip(rel, -128, 128) + 128
    g = np.arange(4)[:, None, None]
    e = np.arange(2)[None, :, None]
    jj = np.arange(2)[None, None, :]
    h = 4 * g + 2 * jj + e
    out = np.empty((2, 128, 3, 4, 2, 2, 64), np.float32)
    outc = np.empty((2, 128, 4, 2, 2, 64), np.float32)
    for l in range(2):
        tab = rel_bias_b[l]
        out[l] = tab[h[None, None, :, :, :, None], idx[:, :, None, None, None, :]]
        outc[l] = np.broadcast_to(tab[h, 256][None, :, :, :, None], (128, 4, 2, 2, 64))
    slopes = np.array(SLOPES, np.float64)[h]
    alibi = (-slopes[None, None, :, :, :, None] * np.abs(rel).astype(np.float64)[:, :, None, None, None, :])
    alibi = alibi.astype(np.float32)
    return out.reshape(2, 128, 3 * 1024), outc.reshape(2, 128, 1024), alibi.reshape(128, 3 * 1024)


_NC_CACHE = {}


def kernel(x_prompt, x_sample, cache_k_a, cache_v_a, cache_k_b, cache_v_b,
           g_attn, g_mlp, w_qkv_a, g_q_a, g_k_a, sink_a, w_o_a,
           g_kv, w_kv, g_k_b, w_q_b, g_q_b, rel_bias_b, w_o_b, w_up, w_down):
    f = lambda a: np.ascontiguousarray(np.asarray(a, dtype=np.float32))
    x_prompt, x_sample = f(x_prompt), f(x_sample)
    cache_k_a, cache_v_a, cache_k_b, cache_v_b = f(cache_k_a), f(cache_v_a), f(cache_k_b), f(cache_v_b)
    rel_bias_b = f(rel_bias_b)
    gall = np.concatenate([f(g_attn), f(g_mlp), f(g_kv)[None]], axis=0)
    gvec = np.ascontiguousarray(gall.reshape(9, 8, 128).transpose(2, 0, 1)).reshape(128, 72)
    cols = [f(g_q_a)[0], f(g_q_a)[1], f(g_k_a)[0], f(g_k_a)[1], f(g_k_b), f(g_q_b)[0], f(g_q_b)[1]]
    gqk = np.ascontiguousarray(np.stack([np.tile(c, 2) for c in cols], axis=1))
    biasB, biasBc, dist = _bias_tables(rel_bias_b)
    g = np.arange(4)[:, None, None]
    e = np.arange(2)[None, :, None]
    jj = np.arange(2)[None, None, :]
    h = 4 * g + 2 * jj + e
    sk = f(sink_a)[:, h]
    sk_rows = np.concatenate([np.broadcast_to(sk[None, :, :, 1, :], (64, 2, 4, 2)),
                              np.broadcast_to(sk[None, :, :, 0, :], (64, 2, 4, 2))], 0)
    sinktile = np.ascontiguousarray(np.broadcast_to(sk_rows[..., None], (128, 2, 4, 2, 64))).reshape(128, 1024)
    shared = {
        "w_qkv_a": f(w_qkv_a), "w_o_a": f(w_o_a), "w_kv": f(w_kv), "w_q_b": f(w_q_b), "w_o_b": f(w_o_b),
        "w_up": f(w_up), "w_down": f(w_down), "gvec": gvec, "gqk": gqk, "alibi": dist,
        "biasB": biasB, "biasBc": biasBc, "sinktile": sinktile,
    }
    in_maps = []
    for c in range(NCORES):
        sl = slice(4 * c, 4 * c + 4)
        cka = cache_k_a[:, sl].transpose(0, 1, 4, 3, 2)
        ckb = cache_k_b[sl].transpose(0, 3, 2, 1)
        m = dict(shared)
        m["xTp"] = np.ascontiguousarray(x_prompt[c].T)
        m["xTs"] = np.ascontiguousarray(x_sample[sl].reshape(TS, D).T)
        m["ckTa"] = np.ascontiguousarray(np.concatenate([cka, cka], axis=2))
        m["cva"] = np.ascontiguousarray(cache_v_a[:, sl].reshape(2, NSTREAM, 128, 256))
        m["ckTb"] = np.ascontiguousarray(np.concatenate([ckb, ckb], axis=1))
        m["cvb"] = np.ascontiguousarray(cache_v_b[sl].reshape(NSTREAM, 4, 128, 256))
        in_maps.append(m)
    if "nc" not in _NC_CACHE:
        _NC_CACHE["nc"] = build_program()
    nc = _NC_CACHE["nc"]
    res = run_bass_kernel_spmd(nc, in_maps, core_ids=list(range(NCORES)))
    R = res.results
    y_prompt = np.stack([np.asarray(R[c]["yTp"]).T for c in range(NCORES)], axis=0)
    y_sample = np.concatenate([np.asarray(R[c]["yTs"]).T.reshape(NSTREAM, DEC, D) for c in range(NCORES)], axis=0)
    ka_p = np.stack([np.asarray(R[c]["okA_p"]).transpose(0, 3, 2, 1) for c in range(NCORES)], axis=1)
    va_p = np.stack([np.asarray(R[c]["ovA_p"]).reshape(2, 128, 4, 64) for c in range(NCORES)], axis=1)
    kb_p = np.stack([np.asarray(R[c]["okB_p"]).transpose(2, 1, 0) for c in range(NCORES)], axis=0)
    vb_p = np.stack([np.asarray(R[c]["ovB_p"]).reshape(512, 4, 64) for c in range(NCORES)], axis=0)
    ka_s = np.concatenate([np.asarray(R[c]["okA_s"]).transpose(0, 3, 2, 1).reshape(2, NSTREAM, DEC, 4, 64)
                           for c in range(NCORES)], axis=1)
    va_s = np.concatenate([np.asarray(R[c]["ovA_s"]).reshape(2, NSTREAM, DEC, 4, 64) for c in range(NCORES)], axis=1)
    kb_s = np.concatenate([np.asarray(R[c]["okB_s"]).transpose(2, 1, 0).reshape(NSTREAM, DEC, 4, 64)
                           for c in range(NCORES)], axis=0)
    vb_s = np.concatenate([np.asarray(R[c]["ovB_s"]).reshape(NSTREAM, DEC, 4, 64) for c in range(NCORES)], axis=0)
    c32 = lambda a: np.ascontiguousarray(a, dtype=np.float32)
    return (c32(y_prompt), c32(y_sample), c32(ka_p), c32(va_p), c32(kb_p), c32(vb_p),
            c32(ka_s), c32(va_s), c32(kb_s), c32(vb_s))
```

```python
import numpy as np
import concourse.bass as bass
import concourse.mybir as mybir
from concourse.bass_utils import run_bass_kernel_spmd

F32 = mybir.dt.float32
BF16 = mybir.dt.bfloat16
ALU = mybir.AluOpType
AF = mybir.ActivationFunctionType

NCORES = 8
D = 1024
SEQ = 8192
T = 512
NT = SEQ // T
TS = 64
NSTREAM = 4
DEC = 16
EPS = 1e-6
NW = 4
NT_RUN = NT
import os as _os
DBG_NOLAG = bool(_os.environ.get('DBG_NOLAG'))
DBG_NOMERGE = bool(_os.environ.get('DBG_NOMERGE'))
DBG_NOPAIR = bool(_os.environ.get('DBG_NOPAIR'))
DO_SAMPLE = True

SLOPES = [2.0 ** (-8.0 * (h + 1) / 16.0) for h in range(16)]


class Op:
    __slots__ = ("eng", "fn", "deps", "chan", "ordinal", "target", "idx", "bulk", "raw")

    def __init__(self, eng, fn, chan=None, bulk=False):
        self.eng = eng
        self.fn = fn
        self.deps = ()
        self.chan = chan
        self.ordinal = 0
        self.target = False
        self.idx = 0
        self.bulk = bulk
        self.raw = ()


class Sched:
    def __init__(self):
        self.ops = []
        self.lastw = {}
        self.readers = {}
        self.chan_count = {}

    def add(self, eng, fn, reads=(), writes=(), chan=None, bulk=False, excl=()):
        i = len(self.ops)
        op = Op(eng, fn, chan, bulk)
        deps = set()
        lastw, readers = self.lastw, self.readers
        for k in reads:
            w = lastw.get(k)
            if w is not None:
                deps.add(w)
        if excl:
            real = set(deps)
            for k in writes:
                w = lastw.get(k)
                if w is not None:
                    real.add(w)
                r = readers.get(k)
                if r:
                    real.update(r.values())
            op.raw = frozenset(real)
            writes = list(writes) + list(excl)
        else:
            op.raw = None
        for k in writes:
            w = lastw.get(k)
            if w is not None:
                deps.add(w)
            r = readers.get(k)
            if r:
                deps.update(r.values())
        for k in reads:
            r = readers.get(k)
            if r is None:
                r = readers[k] = {}
            r[eng if chan is None else ("dma", i)] = i
        for k in writes:
            lastw[k] = i
            readers[k] = {}
        deps.discard(i)
        if chan is not None:
            c = self.chan_count.get(chan, 0) + 1
            self.chan_count[chan] = c
            op.ordinal = c
        op.deps = tuple(deps)
        self.ops.append(op)
        return i


def build_slot_table():
    slots = []
    for l in range(2):
        slots.append(("cols", "w_qkv_a", l, 0, 512))
        slots.append(("cols", "w_qkv_a", l, 512, 512))
        slots.append(("kdup", "w_qkv_a", l, 1024, 256))
        slots.append(("cols", "w_qkv_a", l, 1280, 256))
        slots.append(("cols", "w_o_a", l, 0, 512))
        slots.append(("cols", "w_o_a", l, 512, 512))
        for s in range(8):
            slots.append(("cols", "w_up", l, s * 512, 512))
        for mh in range(2):
            for s in range(4):
                slots.append(("down", "w_down", l, mh, s))
    for j in range(2):
        slots.append(("cols", "w_q_b", j, 0, 512))
        slots.append(("cols", "w_q_b", j, 512, 512))
        if j == 0:
            slots.append(("kdup", "w_kv", None, 0, 256))
            slots.append(("cols", "w_kv", None, 256, 256))
        slots.append(("cols", "w_o_b", j, 0, 512))
        slots.append(("cols", "w_o_b", j, 512, 512))
        for s in range(8):
            slots.append(("cols", "w_up", 2 + j, s * 512, 512))
        for mh in range(2):
            for s in range(4):
                slots.append(("down", "w_down", 2 + j, mh, s))
    return slots


SLOTS = build_slot_table()
NSLOT = len(SLOTS)
PREP_GROUP_BOUNDS = [6, 22, 44, 66, NSLOT]


def prep_group(slot):
    for gi, b in enumerate(PREP_GROUP_BOUNDS):
        if slot < b:
            return gi
    return len(PREP_GROUP_BOUNDS) - 1


def build_program():
    nc = bass.Bass("TRN2", target_bir_lowering=False)

    def din(name, shape, dt=F32):
        return nc.dram_tensor(name, list(shape), dt, kind="ExternalInput").ap()

    def dout(name, shape, dt=F32):
        return nc.dram_tensor(name, list(shape), dt, kind="ExternalOutput").ap()

    xTp = din("xTp", [D, SEQ])
    xTs = din("xTs", [D, TS])
    ckTa = din("ckTa", [2, NSTREAM, 128, 4, 128])
    cva = din("cva", [2, NSTREAM, 128, 256])
    ckTb = din("ckTb", [NSTREAM, 128, 4, 512])
    cvb = din("cvb", [NSTREAM, 4, 128, 256])
    W = {
        "w_qkv_a": din("w_qkv_a", [2, D, 1536]),
        "w_o_a": din("w_o_a", [2, D, D]),
        "w_kv": din("w_kv", [D, 512]),
        "w_q_b": din("w_q_b", [2, D, D]),
        "w_o_b": din("w_o_b", [2, D, D]),
        "w_up": din("w_up", [4, D, 4096]),
        "w_down": din("w_down", [4, 4096, D]),
    }
    gvec_d = din("gvec", [128, 9 * 8])
    gqk_d = din("gqk", [128, 7])
    alibi_d = din("alibi", [128, 3 * 1024])
    biasB_d = din("biasB", [2, 128, 3 * 1024])
    biasBc_d = din("biasBc", [2, 128, 1024])
    sink_d = din("sinktile", [128, 1024])

    yTp = dout("yTp", [D, SEQ])
    yTs = dout("yTs", [D, TS])
    okA_p = dout("okA_p", [2, 64, 4, 128])
    ovA_p = dout("ovA_p", [2, 128, 256])
    okB_p = dout("okB_p", [64, 4, 512])
    ovB_p = dout("ovB_p", [512, 256])
    okA_s = dout("okA_s", [2, 64, 4, TS])
    ovA_s = dout("ovA_s", [2, TS, 256])
    okB_s = dout("okB_s", [64, 4, TS])
    ovB_s = dout("ovB_s", [TS, 256])

    scr = nc.dram_tensor("wscr", [NSLOT, 128, 4096], BF16, kind="Internal").ap()

    S = Sched()
    import contextlib
    es = contextlib.ExitStack()

    def sb(name, shape, dt):
        return es.enter_context(nc.sbuf_tensor("sb_" + name, list(shape), dt))

    xT = sb("xT", [128, 8, T], F32)
    xn = sb("xn", [128, 8, T], BF16)
    big = sb("big", [128, 32, T], BF16)
    qf = sb("qf", [128, 3, T], F32)
    sqr = sb("sqr", [128, 5, T], BF16)
    rs = sb("rs", [128, 2, T], F32)
    rl = sb("rl", [128, 2, T], F32)
    sc = sb("sc", [128, 3, 256], F32)
    rc = sb("rc", [128, 2, T], F32)
    yst = sb("yst", [128, 2, T], F32)
    vst = sb("vst", [128, 2, 256], F32)
    kTA = [sb("kTA%d" % l, [128, 4, 128 + T], BF16) for l in range(2)]
    vA = [sb("vA%d" % l, [128, 5, 576], BF16) for l in range(2)]
    kTB = sb("kTB", [128, 4, 512 + T], BF16)
    vB = sb("vB", [128, 8, 576], BF16)
    biasB = sb("biasB", [128, 2, 3, 4, 2, 2, 64], F32)
    alibi = sb("alibi", [128, 3, 4, 2, 2, 64], F32)
    esink = sb("esink", [128, 2, 4, 2, 64], F32)
    pT = sb("pT", [128, 8, 2, 128], BF16)
    wsl = sb("wsl", [128, NW, 8, 512], BF16)
    ones_bf = sb("ones_bf", [128, 128], BF16)
    blk_bf = sb("blk_bf", [128, 128], BF16)
    selE = sb("selE", [1, 128], BF16)
    selO = sb("selO", [1, 128], BF16)
    gvec = sb("gvec", [128, 9, 8], F32)
    gqk = sb("gqk", [128, 7], F32)
    PS = es.enter_context(nc.psum_tensor("PS", [128, 8, 4, 128], F32))

    def psk(bank):
        return [("psb", bank)]

    def bigk(i):
        return ("big", i)

    def bigk_all(i):
        if 8 <= i < 16:
            return [("big", i), ("oT", i - 8, 0), ("oT", i - 8, 1)]
        return [("big", i)]

    mm_ring = [4, 5, 6, 7, 0, 1, 2]
    NRING = 7
    STATS_BANK = 3
    lagq = []

    lag_keep = [0]

    def run_lagged(keep=0):
        n_ = max(0, len(lagq) - keep)
        q_ = lagq[:n_]
        del lagq[:n_]
        for f_ in q_:
            f_()
    ring_state = {"mm": 0, "qf": 0, "rs": 0, "rl": 0, "sc": 0, "rc": 0, "yst": 0, "kst": 0,
                  "vst": 0, "sq2": 0, "sqr": 0, "S": 0, "pT": 0, "pv": 0}

    def nxt(name, n):
        v = ring_state[name]
        ring_state[name] = v + 1
        return v % n

    def ps_bank_ap(bank, rows=128, n=T):
        return PS[0:rows, bank].rearrange("p a b -> p (a b)")[:, 0:n]

    CONST = "const"
    S.add("sp", lambda e: e.dma_start(out=gvec[:].rearrange("p a b -> p (a b)"), in_=gvec_d[:, :]),
          writes=["gvec"], chan=CONST, bulk=True)
    S.add("sp", lambda e: e.dma_start(out=gqk[:], in_=gqk_d[:, :]), writes=["gqk"], chan=CONST, bulk=True)
    S.add("sp", lambda e: e.dma_start(out=alibi[:].rearrange("p t g e j q -> p (t g e j q)"), in_=alibi_d[:, :]),
          writes=["alibi"], chan=CONST, bulk=True)
    for l in range(2):
        S.add("sp", lambda e, l=l: e.dma_start(
            out=biasB[:, l].rearrange("p t g e j q -> p (t g e j q)"), in_=biasB_d[l]),
            writes=[("biasB", l)], chan=CONST, bulk=True)

    S.add("dve", lambda e: e.memset(ones_bf[:], 1.0), writes=["ones"])
    S.add("dve", lambda e: e.memset(blk_bf[:], 0.0), writes=["blk"])
    S.add("dve", lambda e: e.memset(blk_bf[0:64, 0:64], 1.0), writes=["blk"])
    S.add("dve", lambda e: e.memset(blk_bf[64:128, 64:128], 1.0), writes=["blk"])
    S.add("dve", lambda e: e.memset(selE[:, 0:64], 0.0), writes=["sel"])
    S.add("dve", lambda e: e.memset(selE[:, 64:128], 1.0), writes=["sel"])
    S.add("dve", lambda e: e.memset(selO[:, 0:64], 1.0), writes=["sel"])
    S.add("dve", lambda e: e.memset(selO[:, 64:128], 0.0), writes=["sel"])
    for l in range(2):
        S.add("dve", lambda e, l=l: e.memset(vA[l][:].rearrange("p a b -> p (a b)"), 1.0),
              writes=[("vA", l, b) for b in range(5)])
    S.add("dve", lambda e: e.memset(vB[:].rearrange("p a b -> p (a b)"), 1.0),
          writes=[("vB", b) for b in range(8)])
    S.add("dve", lambda e: e.tensor_scalar(out=gqk[:, 0:2], in0=gqk[:, 0:2], scalar1=0.125, scalar2=None,
                                           op0=ALU.mult), reads=["gqk"], writes=["gqk"])
    S.add("dve", lambda e: e.tensor_scalar(out=gqk[:, 5:7], in0=gqk[:, 5:7], scalar1=0.125, scalar2=None,
                                           op0=ALU.mult), reads=["gqk"], writes=["gqk"])
    rl_flat = rl[:].rearrange("p a b -> p (a b)")
    S.add("sp", lambda e: e.dma_start(out=rl_flat, in_=sink_d[:, :]), writes=[("rl", 0), ("rl", 1)], chan="bc")
    S.add("act", lambda e: e.activation(out=esink[:].rearrange("p l g j q -> p (l g j q)"), in_=rl_flat, func=AF.Exp),
          reads=[("rl", 0), ("rl", 1)], writes=["esink"])
    for l in range(2):
        S.add("sp", lambda e, l=l: e.dma_start(out=rl_flat, in_=biasBc_d[l]),
              writes=[("rl", 0), ("rl", 1)], chan="bc")
        for t in range(3):
            S.add("dve", lambda e, l=l, t=t: e.tensor_tensor(
                out=biasB[:, l, t].rearrange("p g e j q -> p (g e j q)"),
                in0=biasB[:, l, t].rearrange("p g e j q -> p (g e j q)"),
                in1=rl_flat, op=ALU.subtract),
                reads=[("rl", 0), ("rl", 1), ("biasB", l)], writes=[("biasB", l)])

    for si, sd in enumerate(SLOTS):
        ch = ("prep", prep_group(si))
        kind, wname, l = sd[0], sd[1], sd[2]
        wap = W[wname] if l is None else W[wname][l]
        if kind == "cols":
            c0, w = sd[3], sd[4]
            src = wap.rearrange("(kc p) n -> p kc n", p=128)[:, :, c0:c0 + w]
            dst = scr[si].rearrange("p (kc n) -> p kc n", kc=8)[:, :, 0:w]
            S.add("pool", lambda e, src=src, dst=dst: e.dma_start(out=dst, in_=src),
                  writes=[("scr", si)], chan=ch, bulk=True)
        elif kind == "down":
            mh, s = sd[3], sd[4]
            src = wap.rearrange("(s kc p) n -> s p kc n", kc=8, p=128)[s][:, :, mh * 512:(mh + 1) * 512]
            dst = scr[si].rearrange("p (kc n) -> p kc n", kc=8)
            S.add("pool", lambda e, src=src, dst=dst: e.dma_start(out=dst, in_=src),
                  writes=[("scr", si)], chan=ch, bulk=True)
        else:
            c0 = sd[3]
            srcv = wap.rearrange("(kc p) n -> p kc n", p=128)
            dstv = scr[si].rearrange("p (kc g two d) -> p kc g two d", kc=8, g=4, two=2)
            for kc in range(8):
                for dup in range(2):
                    src = srcv[:, kc, c0:c0 + 256].rearrange("p (g d) -> p g d", g=4)
                    dst = dstv[:, kc, :, dup, :]
                    S.add("pool", lambda e, src=src, dst=dst: e.dma_start(out=dst, in_=src),
                          writes=[("scr", si, kc, dup)], chan=ch, bulk=True)

    wstate = {"issued": 0, "used": 0}
    n_tiles_total = NT_RUN + (1 if DO_SAMPLE else 0)
    total_uses = n_tiles_total * NSLOT

    def issue_weight_load(u):
        slot = u % NSLOT
        buf = u % NW
        sd = SLOTS[slot]
        w = sd[4] if sd[0] == "cols" else 512
        if sd[0] == "kdup":
            w = 512
        src = scr[slot].rearrange("p (kc n) -> p kc n", kc=8)[:, :, 0:w]
        dst = wsl[:, buf, :, 0:w]
        rk = [("scr", slot)]
        if sd[0] == "kdup":
            rk = [("scr", slot, kc, dup) for kc in range(8) for dup in range(2)]
        S.add("sp", lambda e, src=src, dst=dst: e.dma_start(out=dst, in_=src),
              reads=rk, writes=[("w", buf)], chan=("w", buf))

    def next_weights(expect_kind=None):
        u = wstate["used"]
        while wstate["issued"] < min(u + NW, total_uses):
            issue_weight_load(wstate["issued"])
            wstate["issued"] += 1
        wstate["used"] = u + 1
        assert expect_kind is None or SLOTS[u % NSLOT][0] == expect_kind
        return u % NW

    cur = {"rs": 0}
    qk_pool = [False]

    def norm_chunk(kc, N, use_pool, nidx, immediate=False):
        S.add("dve", lambda e: e.tensor_scalar(out=xn[:, kc, 0:N], in0=xT[:, kc, 0:N],
                                               scalar1=gvec[:, nidx, kc:kc + 1], scalar2=None, op0=ALU.mult),
              reads=[("x", kc), "gvec"], writes=[("xn", kc)])
        i = nxt("sqr", 5)
        eng = "pool" if use_pool else "dve"
        S.add(eng, lambda e: e.tensor_tensor(out=sqr[:, i, 0:N], in0=xT[:, kc, 0:N], in1=xT[:, kc, 0:N], op=ALU.mult),
              reads=[("x", kc)], writes=[("sqr", i)])

        def mm():
            S.add("pe", lambda e: e.matmul(ps_bank_ap(STATS_BANK, 128, N), ones_bf[:, :], sqr[:, i, 0:N],
                                           start=(kc == 0), stop=(kc == 7)),
                  reads=[("sqr", i), "ones"], excl=psk(STATS_BANK))
        if immediate:
            mm()
        else:
            lagq.append(mm)

    def rmsnorm(N):
        run_lagged()
        r = nxt("rs", 2)
        cur["rs"] = r
        S.add("act", lambda e: e.activation(out=rs[:, r, 0:N], in_=ps_bank_ap(STATS_BANK, 128, N), func=AF.Ln,
                                            scale=1.0 / D, bias=EPS),
              excl=psk(STATS_BANK), writes=[("rs", r)])
        S.add("act", lambda e: e.activation(out=rs[:, r, 0:N], in_=rs[:, r, 0:N], func=AF.Exp, scale=-0.5),
              reads=[("rs", r)], writes=[("rs", r)])

    def rescale_xg(nidx, N):
        for kc in range(8):
            S.add("dve", lambda e, kc=kc: e.tensor_scalar(out=xn[:, kc, 0:N], in0=xT[:, kc, 0:N],
                                                          scalar1=gvec[:, nidx, kc:kc + 1], scalar2=None, op0=ALU.mult),
                  reads=[("x", kc), "gvec"], writes=[("xn", kc)])

    def make_xv(N):
        r = cur["rs"]
        for kc in range(8):
            S.add("dve", lambda e, kc=kc: e.tensor_tensor(out=big[:, 16 + kc, 0:N], in0=xn[:, kc, 0:N],
                                                          in1=rs[:, r, 0:N], op=ALU.mult),
                  reads=[("xn", kc), ("rs", r)], writes=[bigk(16 + kc)])

    def qk_norm(bank, N, gcol, dst_ap, dst_keys, kout=None, fold=True):
        q = nxt("qf", 3)
        r0_ = cur["rs"]
        if fold:
            S.add("dve", lambda e: e.tensor_tensor(out=qf[:, q, 0:N], in0=ps_bank_ap(bank, 128, N), in1=rs[:, r0_, 0:N],
                                                   op=ALU.mult),
                  reads=[("rs", r0_)], excl=psk(bank), writes=[("qf", q)])
        else:
            S.add("act", lambda e: e.activation(out=qf[:, q, 0:N], in_=ps_bank_ap(bank, 128, N), func=AF.Copy),
                  excl=psk(bank), writes=[("qf", q)])
        s2 = 24 + nxt("sq2", 4)
        S.add("pool" if qk_pool[0] else "dve",
              lambda e: e.tensor_tensor(out=big[:, s2, 0:N], in0=qf[:, q, 0:N], in1=qf[:, q, 0:N], op=ALU.mult),
              reads=[("qf", q)], writes=[bigk(s2)])
        def stage2():
            b2 = mm_ring[nxt("mm", NRING)]
            S.add("pe", lambda e: e.matmul(ps_bank_ap(b2, 128, N), blk_bf[:, :], big[:, s2, 0:N], start=True, stop=True),
                  reads=[bigk(s2), "blk"], excl=psk(b2))
            r = nxt("rl", 2)
            S.add("act", lambda e: e.activation(out=rl[:, r, 0:N], in_=ps_bank_ap(b2, 128, N), func=AF.Ln,
                                                scale=1.0 / 64, bias=EPS),
                  excl=psk(b2), writes=[("rl", r)])
            S.add("act", lambda e: e.activation(out=rl[:, r, 0:N], in_=rl[:, r, 0:N], func=AF.Exp, scale=-0.5),
                  reads=[("rl", r)], writes=[("rl", r)])
            S.add("dve", lambda e: e.scalar_tensor_tensor(
                out=dst_ap, in0=qf[:, q, 0:N], scalar=gqk[:, gcol:gcol + 1], in1=rl[:, r, 0:N],
                op0=ALU.mult, op1=ALU.mult),
                reads=[("qf", q), ("rl", r), "gqk"], writes=dst_keys)
            if kout is not None:
                out_ap, c0, c1 = kout
                k = nxt("yst", 2)
                S.add("dve", lambda e: e.scalar_tensor_tensor(
                    out=yst[:, k, 0:N], in0=qf[:, q, 0:N], scalar=gqk[:, gcol:gcol + 1], in1=rl[:, r, 0:N],
                    op0=ALU.mult, op1=ALU.mult),
                    reads=[("qf", q), ("rl", r), "gqk"], writes=[("yst", k)])
                S.add("sp", lambda e: e.dma_start(out=out_ap, in_=yst[0:64, k, c0:c1]),
                      reads=[("yst", k)], chan=("yst", k))
        lag_keep[0] = 1
        lagq.append(stage2)

    def proj_fm(wbuf, mcol, rhs_fn, N, nk=8):
        bank = mm_ring[nxt("mm", NRING)]
        for kc in range(nk):
            S.add("pe", lambda e, kc=kc: e.matmul(ps_bank_ap(bank, 128, N), wsl[:, wbuf, kc, mcol:mcol + 128],
                                                  rhs_fn(kc)[0], start=(kc == 0), stop=(kc == nk - 1)),
                  reads=[("w", wbuf)] + rhs_fn(kc)[1], excl=psk(bank))
        run_lagged(lag_keep[0])
        return bank

    def proj_group_kc_outer(wbuf, rhs_fn, N, nk=8):
        import os
        if os.environ.get("DBG_NO_KCO"):
            return [proj_fm(wbuf, mm * 128, rhs_fn, N, nk) for mm in range(4)]
        banks = [mm_ring[nxt("mm", NRING)] for _ in range(4)]
        for kc in range(nk):
            for mm in range(4):
                S.add("pe", lambda e, kc=kc, mm=mm: e.matmul(ps_bank_ap(banks[mm], 128, N),
                                                             wsl[:, wbuf, kc, mm * 128:(mm + 1) * 128],
                                                             rhs_fn(kc)[0], start=(kc == 0), stop=(kc == nk - 1)),
                      reads=[("w", wbuf)] + rhs_fn(kc)[1], excl=psk(banks[mm]))
        run_lagged()
        return banks

    def xn_rhs(N):
        return lambda kc: (xn[:, kc, 0:N], [("xn", kc)])

    def v_tok(wbuf, N, vbuf, vkey, blk0, sample, vout):
        lag_keep[0] = 0
        run_lagged()
        if not sample:
            groups = [(tb * 128, 128, blk0 + tb) for tb in range(N // 128)]
        else:
            groups = [(s * DEC, DEC, blk0 + s) for s in range(NSTREAM)]
        for gi, (t0, M, blk) in enumerate(groups):
            bank = mm_ring[nxt("mm", NRING)]
            for kc in range(8):
                S.add("pe", lambda e, kc=kc, t0=t0, M=M, bank=bank: e.matmul(
                    ps_bank_ap(bank, M, 256), big[:, 16 + kc, t0:t0 + M], wsl[:, wbuf, kc, 0:256],
                    start=(kc == 0), stop=(kc == 7)),
                    reads=[("w", wbuf), bigk(16 + kc)], excl=psk(bank))
            dstv = vbuf[0:M, blk, 64:576].rearrange("p (g c) -> p g c", g=4)[:, :, 0:64]
            S.add("act", lambda e, dstv=dstv, bank=bank, M=M: e.activation(
                out=dstv, in_=ps_bank_ap(bank, M, 256).rearrange("p (g d) -> p g d", g=4), func=AF.Copy),
                excl=psk(bank), writes=[vkey(blk)])
            if vout is not None:
                o = vout(gi, t0, M)
                if o is not None:
                    k = nxt("vst", 2)
                    S.add("dve", lambda e, k=k, bank=bank, M=M: e.tensor_copy(out=vst[0:M, k, :],
                                                                                in_=ps_bank_ap(bank, M, 256)),
                          excl=psk(bank), writes=[("vst", k)])
                    S.add("sp", lambda e, k=k, o=o, M=M: e.dma_start(out=o, in_=vst[0:M, k, :]),
                          reads=[("vst", k)], chan=("vst", k))

    def residual_add(bank, m, N, to_out=None, nxt_norm=None, nidx=None):
        if to_out is None:
            S.add("dve", lambda e: e.tensor_tensor(out=xT[:, m, 0:N], in0=ps_bank_ap(bank, 128, N),
                                                   in1=xT[:, m, 0:N], op=ALU.add),
                  excl=psk(bank), reads=[("x", m)], writes=[("x", m)])
            if nidx is not None:
                norm_chunk(m, N, nxt_norm, nidx)
        else:
            y = nxt("yst", 2)
            S.add("dve", lambda e: e.tensor_tensor(out=yst[:, y, 0:N], in0=ps_bank_ap(bank, 128, N),
                                                   in1=xT[:, m, 0:N], op=ALU.add),
                  excl=psk(bank), reads=[("x", m)], writes=[("yst", y)])
            S.add("sp", lambda e: e.dma_start(out=to_out(m), in_=yst[:, y, 0:N]),
                  reads=[("yst", y)], chan=("yst", y))

    def wo_proj(N, nxt_norm, nidx):
        for half in range(2):
            wb = next_weights("cols")
            for mm in range(4):
                m = half * 4 + mm
                bank = proj_fm(wb, mm * 128, lambda kc: (big[:, 8 + kc, 0:N], bigk_all(8 + kc)), N)
                residual_add(bank, m, N, None, nxt_norm, nidx)

    def relu2_evac(bank, m, N):
        r = nxt("rl", 2)
        r0_ = cur["rs"]
        S.add("dve", lambda e: e.scalar_tensor_tensor(out=rl[:, r, 0:N], in0=ps_bank_ap(bank, 128, N), scalar=0.0,
                                                      in1=rs[:, r0_, 0:N], op0=ALU.max, op1=ALU.mult),
              reads=[("rs", r0_)], excl=psk(bank), writes=[("rl", r)])
        S.add("act", lambda e: e.activation(out=big[:, m, 0:N], in_=rl[:, r, 0:N], func=AF.Square),
              reads=[("rl", r)], writes=bigk_all(m))

    def mlp(N, to_out=None, nxt_norm=None, nidx=None, after_chunk=None):
        for s in range(8):
            wb = next_weights("cols")
            for mm in range(4):
                m = s * 4 + mm
                bank = proj_fm(wb, mm * 128, xn_rhs(N), N)
                relu2_evac(bank, m, N)
        for mh in range(2):
            acc = [4, 5, 6, 7] if mh == 0 else [0, 1, 2, 4]
            for s in range(4):
                wb = next_weights("down")
                for mm in range(4):
                    for kc in range(8):
                        S.add("pe", lambda e, mm=mm, kc=kc, s=s, wb=wb, acc=acc: e.matmul(
                            ps_bank_ap(acc[mm], 128, N), wsl[:, wb, kc, mm * 128:(mm + 1) * 128],
                            big[:, s * 8 + kc, 0:N], start=(s == 0 and kc == 0), stop=(s == 3 and kc == 7)),
                            reads=[("w", wb)] + bigk_all(s * 8 + kc), excl=psk(acc[mm]))
                run_lagged()
            for mm in range(4):
                residual_add(acc[mm], mh * 4 + mm, N, to_out, nxt_norm, nidx)
                if after_chunk is not None:
                    after_chunk(mh * 4 + mm)

    def pieces_for(cc, L, first_valid_blk, lo_rows):
        out = []
        if cc % 2 == 0:
            nfull = L // 2
            b0 = (cc - L) // 2
            for k in range(nfull):
                t = 1 if k == nfull - 1 else None
                out.append((b0 + k, 128, [(0, 128, t)]))
            out.append((cc // 2, lo_rows, [(0, lo_rows, 0)]))
        else:
            nfull = L // 2
            bh = (cc - L - 1) // 2
            out.append((bh, 128, [(64, 128, 0 if L == 2 else None)]))
            b0 = (cc - L + 1) // 2
            for k in range(nfull):
                if k == nfull - 1:
                    out.append((b0 + k, 128, [(0, 128, 2)]))
                elif k == nfull - 2:
                    out.append((b0 + k, 128, [(0, 64, None), (64, 128, 0)]))
                else:
                    out.append((b0 + k, 128, [(0, 128, None)]))
        return [p for p in out if p[0] >= first_valid_blk]

    def attention(iters, is_a, lidx):
        hooks = {"begin_now": [], "begin_next": [], "end_now": [], "end_next": []}
        flat = []
        for ii, it in enumerate(iters):
            it["slots"] = [None] * len(it["pieces"])
            for pi in range(len(it["pieces"])):
                flat.append((ii, pi))
        groups = [flat[k:k + 4] for k in range(0, len(flat), 4)]
        last_group_of_iter = {}
        for gi, grp in enumerate(groups):
            for (ii, pi) in grp:
                last_group_of_iter[ii] = gi

        def qk_group(gi):
            sset = nxt("S", 2)
            for qi, (ii, pi) in enumerate(groups[gi]):
                it = iters[ii]
                g, c0, nq = it["g"], it["c0"], it["nq"]
                blk, M, subs = it["pieces"][pi]
                ptl = nxt("pT", 8)
                it["slots"][pi] = (sset, qi, ptl)
                kc0 = it["keycol"](blk)
                for e_ in range(2):
                    bank = e_ * 2 + sset
                    S.add("pe", lambda e, e_=e_, bank=bank, qi=qi, kc0=kc0, M=M, it=it, g=g, c0=c0, nq=nq: e.matmul(
                        PS[0:M, bank, qi, 0:2 * nq],
                        it["kT"][e_ * 64:(e_ + 1) * 64, g, kc0:kc0 + M],
                        big[e_ * 64:(e_ + 1) * 64, 2 * g:2 * g + 2, c0:c0 + nq], start=True, stop=True),
                        reads=[it["kTkey"](blk, g), bigk(2 * g), bigk(2 * g + 1)], excl=psk(bank))

        def exp_group(gi):
            grp = groups[gi]
            k = 0
            while k < len(grp):
                ii, pi = grp[k]
                it = iters[ii]
                g, c0, nq = it["g"], it["c0"], it["nq"]
                blk, M, subs = it["pieces"][pi]
                sset, qi, ptl = it["slots"][pi]
                sbanks = psk(sset) + psk(2 + sset)
                if subs == [(0, 128, None)]:
                    n_ = 1
                    while k + n_ < len(grp) and not DBG_NOMERGE:
                        ii2, pi2 = grp[k + n_]
                        it2 = iters[ii2]
                        if it2["pieces"][pi2][2] != [(0, 128, None)] or it2["nq"] != nq:
                            break
                        if it2["slots"][pi2][2] != ptl + n_:
                            break
                        n_ += 1
                    sview = PS[0:128, sset:4:2, qi:qi + n_, 0:2 * nq]
                    oview = pT[0:128, ptl:ptl + n_, :, 0:2 * nq].rearrange("p s e c -> p e s c")
                    S.add("act", lambda e, sview=sview, oview=oview: e.activation(out=oview, in_=sview, func=AF.Exp),
                          excl=sbanks, writes=[("pT", ptl + t_) for t_ in range(n_)])
                    k += n_
                    continue
                for (r0, r1, bt) in subs:
                    sview = PS[r0:r1, sset:4:2, qi, 0:2 * nq]
                    oview = pT[r0:r1, ptl, :, 0:2 * nq]
                    if bt is None:
                        S.add("act", lambda e, sview=sview, oview=oview: e.activation(out=oview, in_=sview, func=AF.Exp),
                              excl=sbanks, writes=[("pT", ptl)])
                        continue
                    s_ = nxt("sc", 3)
                    scv = sc[r0:r1, s_, 0:4 * nq].rearrange("p (e n) -> p e n", e=2)
                    btile = alibi if is_a else biasB[:, lidx]
                    bkey = "alibi" if is_a else ("biasB", lidx)
                    S.add("dve", lambda e, sview=sview, scv=scv, r0=r0, r1=r1, bt=bt, g=g, nq=nq, btile=btile: e.tensor_tensor(
                        out=scv.rearrange("p e (j q) -> p e j q", j=2),
                        in0=sview.rearrange("p e (j q) -> p e j q", j=2),
                        in1=btile[r0:r1, bt, g, :, :, 0:nq], op=ALU.add),
                        reads=[bkey], excl=sbanks, writes=[("sc", s_)])
                    S.add("act", lambda e, scv=scv, oview=oview: e.activation(out=oview, in_=scv, func=AF.Exp),
                          reads=[("sc", s_)], writes=[("pT", ptl)])
                k += 1

        def pv_phase(it):
            g, c0, nq = it["g"], it["c0"], it["nq"]
            idx_ = pvc[0]
            pvc[0] += 1
            pvs, qd = (idx_ // 4) % 2, idx_ % 4
            bE, bO = 4 + 2 * pvs, 5 + 2 * pvs
            npieces = len(it["pieces"])
            for pi, (blk, M, subs) in enumerate(it["pieces"]):
                sset, qi, ptl = it["slots"][pi]
                r0 = min(s_[0] for s_ in subs)
                r1 = max(s_[1] for s_ in subs)
                vb = it["vblk"](blk)
                for e_ in range(2):
                    vc0 = 64 + 128 * g if e_ == 0 else 128 * g
                    last = (pi == npieces - 1)
                    bank = bE if e_ == 0 else bO
                    S.add("pe", lambda e, e_=e_, r0=r0, r1=r1, vb=vb, vc0=vc0, ptl=ptl, pi=pi, last=last, it=it, nq=nq, bank=bank:
                          e.matmul(PS[0:128, bank, qd, 0:2 * nq], it["vbuf"][r0:r1, vb, vc0:vc0 + 128],
                                   pT[r0:r1, ptl, e_, 0:2 * nq], start=(pi == 0), stop=last),
                          reads=[it["vkey"](vb), ("pT", ptl)], excl=psk(bank))
            it["pvb"] = (bE, bO, qd)

        def pv_norm(its):
            nq = its[0]["nq"]
            n_ = len(its)
            bE, bO, _ = its[0]["pvb"]
            assert all(t_["pvb"][0] == bE and t_["pvb"][2] == i_ for i_, t_ in enumerate(its))
            r = nxt("rc", 2)
            W_ = 2 * nq
            if is_a:
                g0 = its[0]["g"]
                assert all(t_["g"] == g0 + i_ for i_, t_ in enumerate(its))
                S.add("dve", lambda e: e.tensor_tensor(
                    out=rc[64:128, r, 0:n_ * W_].rearrange("p (i j q) -> p i j q", i=n_, j=2),
                    in0=PS[64:128, bE, 0:n_, 0:W_].rearrange("p i (j q) -> p i j q", j=2),
                    in1=esink[64:128, lidx, g0:g0 + n_, :, 0:nq], op=ALU.add),
                    reads=["esink"], excl=psk(bE), writes=[("rc", r, "E")])
                S.add("dve", lambda e: e.tensor_tensor(
                    out=rc[0:64, r, 0:n_ * W_].rearrange("p (i j q) -> p i j q", i=n_, j=2),
                    in0=PS[0:64, bO, 0:n_, 0:W_].rearrange("p i (j q) -> p i j q", j=2),
                    in1=esink[0:64, lidx, g0:g0 + n_, :, 0:nq], op=ALU.add),
                    reads=["esink"], excl=psk(bO), writes=[("rc", r, "O")])
                S.add("act", lambda e: e.activation(out=rc[:, r, 0:n_ * W_], in_=rc[:, r, 0:n_ * W_], func=AF.Ln),
                      reads=[("rc", r, "E"), ("rc", r, "O")], writes=[("rc", r, "E"), ("rc", r, "O")])
            else:
                S.add("act", lambda e: e.activation(
                    out=rc[64:128, r, 0:n_ * W_].rearrange("p (i c) -> p i c", i=n_),
                    in_=PS[64:128, bE, 0:n_, 0:W_], func=AF.Ln),
                    excl=psk(bE), writes=[("rc", r, "E")])
                S.add("act", lambda e: e.activation(
                    out=rc[0:64, r, 0:n_ * W_].rearrange("p (i c) -> p i c", i=n_),
                    in_=PS[0:64, bO, 0:n_, 0:W_], func=AF.Ln),
                    excl=psk(bO), writes=[("rc", r, "O")])
            def stage_b():
                S.add("act", lambda e: e.activation(out=rc[:, r, 0:n_ * W_], in_=rc[:, r, 0:n_ * W_],
                                                    func=AF.Exp, scale=-1.0),
                      reads=[("rc", r, "E"), ("rc", r, "O")], writes=[("rc", r, "E"), ("rc", r, "O")])

            def stage_c():
                mults()
            hooks["begin_next"].append(stage_b)
            hooks["end_next"].append(stage_c)

            def mults():
              g0_ = its[0]["g"]
              aligned = all(t_["g"] == g0_ + i_ and t_["c0"] == its[0]["c0"] for i_, t_ in enumerate(its))
              if aligned:
                  c0 = its[0]["c0"]
                  okE = [("oT", 2 * (g0_ + i_) + j_, 0) for i_ in range(n_) for j_ in range(2)]
                  okO = [("oT", 2 * (g0_ + i_) + j_, 1) for i_ in range(n_) for j_ in range(2)]
                  S.add("dve", lambda e: e.tensor_tensor(
                      out=big[0:64, 8 + 2 * g0_:8 + 2 * g0_ + 2 * n_, c0:c0 + nq].rearrange("p (g j) q -> p g j q", j=2),
                      in0=PS[0:64, bE, 0:n_, 0:W_].rearrange("p g (j q) -> p g j q", j=2),
                      in1=rc[64:128, r, 0:n_ * W_].rearrange("p (g j q) -> p g j q", g=n_, j=2), op=ALU.mult),
                      reads=[("rc", r, "E")], excl=psk(bE), writes=okE)
                  S.add("dve", lambda e: e.tensor_tensor(
                      out=big[64:128, 8 + 2 * g0_:8 + 2 * g0_ + 2 * n_, c0:c0 + nq].rearrange("p (g j) q -> p g j q", j=2),
                      in0=PS[64:128, bO, 0:n_, 0:W_].rearrange("p g (j q) -> p g j q", j=2),
                      in1=rc[0:64, r, 0:n_ * W_].rearrange("p (g j q) -> p g j q", g=n_, j=2), op=ALU.mult),
                      reads=[("rc", r, "O")], excl=psk(bO), writes=okO)
                  return
              for i_, it in enumerate(its):
                  g, c0 = it["g"], it["c0"]
                  S.add("dve", lambda e, i_=i_, g=g, c0=c0: e.tensor_tensor(
                      out=big[0:64, 8 + 2 * g:8 + 2 * g + 2, c0:c0 + nq],
                      in0=PS[0:64, bE, i_, 0:W_].rearrange("p (j q) -> p j q", j=2),
                      in1=rc[64:128, r, i_ * W_:(i_ + 1) * W_].rearrange("p (j q) -> p j q", j=2), op=ALU.mult),
                      reads=[("rc", r, "E")], excl=psk(bE), writes=[("oT", 2 * g, 0), ("oT", 2 * g + 1, 0)])
                  S.add("dve", lambda e, i_=i_, g=g, c0=c0: e.tensor_tensor(
                      out=big[64:128, 8 + 2 * g:8 + 2 * g + 2, c0:c0 + nq],
                      in0=PS[64:128, bO, i_, 0:W_].rearrange("p (j q) -> p j q", j=2),
                      in1=rc[0:64, r, i_ * W_:(i_ + 1) * W_].rearrange("p (j q) -> p j q", j=2), op=ALU.mult),
                      reads=[("rc", r, "O")], excl=psk(bO), writes=[("oT", 2 * g, 1), ("oT", 2 * g + 1, 1)])

        lag_keep[0] = 0
        run_lagged()
        ng = len(groups)
        pvc = [0]
        batches = [[]]

        def run_list(name):
            q_ = hooks[name][:]
            del hooks[name][:]
            for f_ in q_:
                f_()

        def flush_hooks():
            run_list("begin_now")
            run_list("begin_next")
            run_list("end_now")
            run_list("end_next")

        for gi in range(ng + 1):
            hooks["begin_now"].extend(hooks["begin_next"])
            del hooks["begin_next"][:]
            run_list("begin_now")
            if gi < ng:
                qk_group(gi)
            if gi >= 1:
                exp_group(gi - 1)
                for ii in range(len(iters)):
                    if last_group_of_iter[ii] == gi - 1:
                        if len(batches[-1]) == 4:
                            flush_hooks()
                            batches.append([])
                        pv_phase(iters[ii])
                        batches[-1].append(iters[ii])
                        if len(batches) >= 2 and len(batches[-1]) == 1:
                            pv_norm(batches.pop(0))
            run_list("end_now")
            hooks["end_now"].extend(hooks["end_next"])
            del hooks["end_next"][:]
        flush_hooks()
        for bt_ in batches:
            if bt_:
                pv_norm(bt_)
        flush_hooks()

    preloaded = [False]

    def emit_x_load(ti, sample, kc):
        if sample:
            src = xTs.rearrange("(kc p) t -> p kc t", p=128)
            n_ = TS
        else:
            src = xTp.rearrange("(kc p) t -> p kc t", p=128)[:, :, ti * T:(ti + 1) * T]
            n_ = T
        S.add("sp", lambda e: e.dma_start(out=xT[:, kc, 0:n_], in_=src[:, kc, :]),
              writes=[("x", kc)], chan=("x", kc))

    def run_tile(ti, sample, nxt=None):
        N = TS if sample else T
        first = (ti == 0) and not sample
        last_prompt = (ti == NT - 1) and not sample
        nq = DEC if sample else 64
        if sample:
            src = xTs.rearrange("(kc p) t -> p kc t", p=128)
        else:
            src = xTp.rearrange("(kc p) t -> p kc t", p=128)[:, :, ti * T:(ti + 1) * T]
        if not preloaded[0]:
            for kc in range(8):
                emit_x_load(ti, sample, kc)
        preloaded[0] = False
        use_pool = sample or ti > 0
        qk_pool[0] = use_pool
        for kc in range(8):
            norm_chunk(kc, N, use_pool, 0, immediate=True)

        def make_iters(kind, l):
            its = []
            if kind == "A":
                kT, vbuf, L, hist_blocks = kTA[l], vA[l], 2, 1
                kTkey = lambda blk, g: ("kTA", l, "hist" if blk < 1 else "cur", g)
                vkey = lambda blk: ("vA", l, blk)
            else:
                kT, vbuf, L, hist_blocks = kTB, vB, 8, 4
                kTkey = lambda blk, g: ("kTB", "hist" if blk < 4 else "cur", g)
                vkey = lambda blk: ("vB", blk)
            if not sample:
                for c in range(8):
                    cc = c + 2 * hist_blocks
                    pcs = pieces_for(cc, L, hist_blocks if first else 0, 64)
                    for g in range(4):
                        its.append(dict(c0=c * 64, nq=64, g=g, pieces=pcs, kT=kT, kTkey=kTkey,
                                        keycol=lambda blk: blk * 128, vbuf=vbuf, vkey=vkey, vblk=lambda blk: blk))
            else:
                for s in range(NSTREAM):
                    cc = 2 * hist_blocks
                    pcs = pieces_for(cc, L, 0, DEC)
                    for g in range(4):
                        d = dict(c0=s * DEC, nq=DEC, g=g, pieces=pcs, kT=kT, kTkey=kTkey,
                                 keycol=(lambda blk, s=s, hb=hist_blocks: blk * 128 if blk < hb else hb * 128 + s * DEC),
                                 vbuf=vbuf, vkey=vkey, stream=s,
                                 vblk=(lambda blk, s=s, hb=hist_blocks: blk if blk < hb else hb + s))
                        its.append(d)
            return its

        def run_attention(kind, l):
            its = make_iters(kind, l)
            if not sample:
                attention(its, kind == "A", l)
            else:
                for s in range(NSTREAM):
                    load_cache(kind, l, s)
                    attention([d for d in its if d["stream"] == s], kind == "A", l)

        def load_cache(kind, l, s):
            if kind == "A":
                S.add("pool", lambda e: e.dma_start(out=kTA[l][:, :, 0:128], in_=ckTa[l, s]),
                      writes=[("kTA", l, "hist", g) for g in range(4)], chan=("ck", "A", l))
                dstv = vA[l][:, 0, 64:576].rearrange("p (g c) -> p g c", g=4)[:, :, 0:64]
                S.add("pool", lambda e: e.dma_start(out=dstv, in_=cva[l, s].rearrange("p (g d) -> p g d", g=4)),
                      writes=[("vA", l, 0)], chan=("cv", "A", l))
            else:
                S.add("pool", lambda e: e.dma_start(out=kTB[:, :, 0:512], in_=ckTb[s]),
                      writes=[("kTB", "hist", g) for g in range(4)], chan=("ck", "B"))
                for b in range(4):
                    dstv = vB[:, b, 64:576].rearrange("p (g c) -> p g c", g=4)[:, :, 0:64]
                    S.add("pool", lambda e, b=b, dstv=dstv: e.dma_start(
                        out=dstv, in_=cvb[s, b].rearrange("p (g d) -> p g d", g=4)),
                        writes=[("vB", b)], chan=("cv", "B", b))

        for l in range(2):
            rmsnorm(N)
            for half in range(2):
                wb = next_weights("cols")
                for mm in range(4):
                    j = half * 4 + mm
                    bank = proj_fm(wb, mm * 128, xn_rhs(N), N)
                    qk_norm(bank, N, l, big[:, j, 0:N], [bigk(j)])
            wb = next_weights("kdup")
            for g in range(4):
                bank = proj_fm(wb, g * 128, xn_rhs(N), N)
                kout = None
                if sample:
                    kout = (okA_s[l, :, g, :], 0, TS)
                elif last_prompt:
                    kout = (okA_p[l, :, g, :], T - 128, T)
                qk_norm(bank, N, 2 + l, kTA[l][:, g, 128:128 + N], [("kTA", l, "cur", g)], kout)
            wb = next_weights("cols")
            make_xv(N)
            if sample:
                vout = lambda gi, t0, M, l=l: ovA_s[l, t0:t0 + M, :]
            elif last_prompt:
                vout = lambda gi, t0, M, l=l: (ovA_p[l, :, :] if gi == 3 else None)
            else:
                vout = None
            v_tok(wb, N, vA[l], lambda blk, l=l: ("vA", l, blk), 1, sample, vout)
            run_attention("A", l)
            wo_proj(N, use_pool, 4 + l)
            rmsnorm(N)
            mlp(N, None, use_pool, 1 if l == 0 else 2)
            if not sample and not last_prompt:
                S.add("pool", lambda e, l=l: e.tensor_copy(out=kTA[l][:, :, 0:128], in_=kTA[l][:, :, T:T + 128]),
                      reads=[("kTA", l, "cur", g) for g in range(4)],
                      writes=[("kTA", l, "hist", g) for g in range(4)])
                S.add("pool", lambda e, l=l: e.tensor_copy(out=vA[l][:, 0, :], in_=vA[l][:, 4, :]),
                      reads=[("vA", l, 4)], writes=[("vA", l, 0)])
        rmsnorm(N)

        def q_proj_b(j2):
            for half in range(2):
                wb = next_weights("cols")
                for mm in range(4):
                    j = half * 4 + mm
                    bank = proj_fm(wb, mm * 128, xn_rhs(N), N)
                    qk_norm(bank, N, 5 + j2, big[:, j, 0:N], [bigk(j)])

        r8 = cur["rs"]
        for kc in range(8):
            S.add("dve", lambda e, kc=kc: e.scalar_tensor_tensor(
                out=big[:, 16 + kc, 0:N], in0=xT[:, kc, 0:N], scalar=gvec[:, 8, kc:kc + 1], in1=rs[:, r8, 0:N],
                op0=ALU.mult, op1=ALU.mult),
                reads=[("x", kc), ("rs", r8), "gvec"], writes=[bigk(16 + kc)])
        q_proj_b(0)
        xv_rhs = lambda kc: (big[:, 16 + kc, 0:N], [bigk(16 + kc)])
        wb = next_weights("kdup")
        for g in range(4):
            bank = proj_fm(wb, g * 128, xv_rhs, N)
            kout = None
            if sample:
                kout = (okB_s[:, g, :], 0, TS)
            elif last_prompt:
                kout = (okB_p[:, g, :], 0, T)
            qk_norm(bank, N, 4, kTB[:, g, 512:512 + N], [("kTB", "cur", g)], kout, fold=False)
        wb = next_weights("cols")
        if sample:
            vout = lambda gi, t0, M: ovB_s[t0:t0 + M, :]
        elif last_prompt:
            vout = lambda gi, t0, M: ovB_p[t0:t0 + M, :]
        else:
            vout = None
        v_tok(wb, N, vB, lambda blk: ("vB", blk), 4, sample, vout)
        for j2 in range(2):
            if j2 == 1:
                rmsnorm(N)
                q_proj_b(1)
            run_attention("B", j2)
            wo_proj(N, use_pool, 6 + j2)
            rmsnorm(N)
            if j2 == 1:
                if sample:
                    to_out = lambda m: yTs[m * 128:(m + 1) * 128, :]
                else:
                    to_out = lambda m, ti=ti: yTp[m * 128:(m + 1) * 128, ti * T:(ti + 1) * T]
                if nxt is not None:
                    mlp(N, to_out, None, None, after_chunk=lambda m: emit_x_load(nxt[0], nxt[1], m))
                    preloaded[0] = True
                else:
                    mlp(N, to_out, None, None)
            else:
                mlp(N, None, use_pool, 3)
        if not sample and not last_prompt:
            S.add("pool", lambda e: e.tensor_copy(out=kTB[:, :, 0:512], in_=kTB[:, :, 512:1024]),
                  reads=[("kTB", "cur", g) for g in range(4)], writes=[("kTB", "hist", g) for g in range(4)])
            S.add("pool", lambda e: e.tensor_copy(out=vB[:, 0:4, :], in_=vB[:, 4:8, :]),
                  reads=[("vB", b) for b in range(4, 8)], writes=[("vB", b) for b in range(4)])

    for ti in range(NT_RUN):
        if ti + 1 < NT_RUN:
            nxt_ = (ti + 1, False)
        elif DO_SAMPLE:
            nxt_ = (0, True)
        else:
            nxt_ = None
        run_tile(ti, False, nxt_)
        run_lagged()
    if DO_SAMPLE:
        run_tile(0, True, None)
        run_lagged()

    ops = S.ops
    ENGS = ["pe", "act", "dve", "pool", "sp"]
    def needs_wait(op, d):
        dop = ops[d]
        if dop.chan is None and op.chan is None and dop.eng == op.eng:
            if dop.eng == "pe":
                return False
            if op.raw is not None and d not in op.raw:
                return False
        return True

    for op in ops:
        for d in op.deps:
            dop = ops[d]
            if dop.chan is None and needs_wait(op, d):
                dop.target = True
    counters = {e: 0 for e in ENGS}
    for op in ops:
        if op.chan is None and op.target:
            counters[op.eng] += 1
            op.idx = counters[op.eng]

    sem_eng = {e: es.enter_context(nc.semaphore("s_" + e)) for e in ["pe", "act", "dve", "pool"]}
    chans = sorted(S.chan_count.keys(), key=str)
    sem_ch = {c: es.enter_context(nc.semaphore("c%d" % i)) for i, c in enumerate(chans)}
    chan_total = dict(S.chan_count)

    def token(dop):
        if dop.chan is None:
            return (("e", dop.eng), dop.idx)
        if dop.bulk:
            return (("c", dop.chan), 16 * chan_total[dop.chan])
        return (("c", dop.chan), 16 * dop.ordinal)

    def sem_of(key):
        return sem_eng[key[1]] if key[0] == "e" else sem_ch[key[1]]

    per_eng = {e: [] for e in ENGS}
    for op in ops:
        per_eng[op.eng].append(op)

    out_chans = [c for c in chans if isinstance(c, tuple) and c[0] in ("yst", "kst", "vst")]

    def emit(engname, eng):
        waited = {}
        for op in per_eng[engname]:
            need = {}
            for d in op.deps:
                dop = ops[d]
                if not needs_wait(op, d):
                    continue
                k, v = token(dop)
                if v > need.get(k, 0):
                    need[k] = v
            for k, v in need.items():
                if waited.get(k, 0) < v:
                    eng.wait_ge(sem_of(k), v)
                    waited[k] = v
            ins = op.fn(eng)
            if op.chan is not None:
                ins.then_inc(sem_ch[op.chan], 16)
            elif op.target:
                ins.then_inc(sem_eng[engname], 1)
        if engname == "sp":
            for c in out_chans:
                eng.wait_ge(sem_ch[c], 16 * chan_total[c])

    block = es.enter_context(nc.Block())

    @block.tensor
    def _(e):
        emit("pe", e)

    @block.scalar
    def _(e):
        emit("act", e)

    @block.vector
    def _(e):
        emit("dve", e)

    @block.gpsimd
    def _(e):
        emit("pool", e)

    @block.sync
    def _(e):
        emit("sp", e)

    es.close()
    return nc


def _bias_tables(rel_bias_b):
    p = np.arange(128)
    ki = (p % 64)[:, None, None]
    half = (p // 64)[:, None, None]
    jt = np.array([[0, 2], [2, 1], [1, 0]])
    t = np.arange(3)[None, :, None]
    j = jt[t, half]
    q = np.arange(64)[None, None, :]
    rel = 64 * j + q - ki
    idx = np.clip(rel, -128, 128) + 128
    g = np.arange(4)[:, None, None]
    e = np.arange(2)[None, :, None]
    jj = np.arange(2)[None, None, :]
    h = 4 * g + 2 * jj + e
    out = np.empty((2, 128, 3, 4, 2, 2, 64), np.float32)
    outc = np.empty((2, 128, 4, 2, 2, 64), np.float32)
    for l in range(2):
        tab = rel_bias_b[l]
        out[l] = tab[h[None, None, :, :, :, None], idx[:, :, None, None, None, :]]
        outc[l] = np.broadcast_to(tab[h, 256][None, :, :, :, None], (128, 4, 2, 2, 64))
    slopes = np.array(SLOPES, np.float64)[h]
    alibi = (-slopes[None, None, :, :, :, None] * np.abs(rel).astype(np.float64)[:, :, None, None, None, :])
    alibi = alibi.astype(np.float32)
    return out.reshape(2, 128, 3 * 1024), outc.reshape(2, 128, 1024), alibi.reshape(128, 3 * 1024)


_NC_CACHE = {}


def kernel(x_prompt, x_sample, cache_k_a, cache_v_a, cache_k_b, cache_v_b,
           g_attn, g_mlp, w_qkv_a, g_q_a, g_k_a, sink_a, w_o_a,
           g_kv, w_kv, g_k_b, w_q_b, g_q_b, rel_bias_b, w_o_b, w_up, w_down):
    f = lambda a: np.ascontiguousarray(np.asarray(a, dtype=np.float32))
    x_prompt, x_sample = f(x_prompt), f(x_sample)
    cache_k_a, cache_v_a, cache_k_b, cache_v_b = f(cache_k_a), f(cache_v_a), f(cache_k_b), f(cache_v_b)
    rel_bias_b = f(rel_bias_b)
    gall = np.concatenate([f(g_attn), f(g_mlp), f(g_kv)[None]], axis=0)
    gvec = np.ascontiguousarray(gall.reshape(9, 8, 128).transpose(2, 0, 1)).reshape(128, 72)
    cols = [f(g_q_a)[0], f(g_q_a)[1], f(g_k_a)[0], f(g_k_a)[1], f(g_k_b), f(g_q_b)[0], f(g_q_b)[1]]
    gqk = np.ascontiguousarray(np.stack([np.tile(c, 2) for c in cols], axis=1))
    biasB, biasBc, dist = _bias_tables(rel_bias_b)
    g = np.arange(4)[:, None, None]
    e = np.arange(2)[None, :, None]
    jj = np.arange(2)[None, None, :]
    h = 4 * g + 2 * jj + e
    sk = f(sink_a)[:, h]
    sk_rows = np.concatenate([np.broadcast_to(sk[None, :, :, 1, :], (64, 2, 4, 2)),
                              np.broadcast_to(sk[None, :, :, 0, :], (64, 2, 4, 2))], 0)
    sinktile = np.ascontiguousarray(np.broadcast_to(sk_rows[..., None], (128, 2, 4, 2, 64))).reshape(128, 1024)
    shared = {
        "w_qkv_a": f(w_qkv_a), "w_o_a": f(w_o_a), "w_kv": f(w_kv), "w_q_b": f(w_q_b), "w_o_b": f(w_o_b),
        "w_up": f(w_up), "w_down": f(w_down), "gvec": gvec, "gqk": gqk, "alibi": dist,
        "biasB": biasB, "biasBc": biasBc, "sinktile": sinktile,
    }
    in_maps = []
    for c in range(NCORES):
        sl = slice(4 * c, 4 * c + 4)
        cka = cache_k_a[:, sl].transpose(0, 1, 4, 3, 2)
        ckb = cache_k_b[sl].transpose(0, 3, 2, 1)
        m = dict(shared)
        m["xTp"] = np.ascontiguousarray(x_prompt[c].T)
        m["xTs"] = np.ascontiguousarray(x_sample[sl].reshape(TS, D).T)
        m["ckTa"] = np.ascontiguousarray(np.concatenate([cka, cka], axis=2))
        m["cva"] = np.ascontiguousarray(cache_v_a[:, sl].reshape(2, NSTREAM, 128, 256))
        m["ckTb"] = np.ascontiguousarray(np.concatenate([ckb, ckb], axis=1))
        m["cvb"] = np.ascontiguousarray(cache_v_b[sl].reshape(NSTREAM, 4, 128, 256))
        in_maps.append(m)
    if "nc" not in _NC_CACHE:
        _NC_CACHE["nc"] = build_program()
    nc = _NC_CACHE["nc"]
    res = run_bass_kernel_spmd(nc, in_maps, core_ids=list(range(NCORES)))
    R = res.results
    y_prompt = np.stack([np.asarray(R[c]["yTp"]).T for c in range(NCORES)], axis=0)
    y_sample = np.concatenate([np.asarray(R[c]["yTs"]).T.reshape(NSTREAM, DEC, D) for c in range(NCORES)], axis=0)
    ka_p = np.stack([np.asarray(R[c]["okA_p"]).transpose(0, 3, 2, 1) for c in range(NCORES)], axis=1)
    va_p = np.stack([np.asarray(R[c]["ovA_p"]).reshape(2, 128, 4, 64) for c in range(NCORES)], axis=1)
    kb_p = np.stack([np.asarray(R[c]["okB_p"]).transpose(2, 1, 0) for c in range(NCORES)], axis=0)
    vb_p = np.stack([np.asarray(R[c]["ovB_p"]).reshape(512, 4, 64) for c in range(NCORES)], axis=0)
    ka_s = np.concatenate([np.asarray(R[c]["okA_s"]).transpose(0, 3, 2, 1).reshape(2, NSTREAM, DEC, 4, 64)
                           for c in range(NCORES)], axis=1)
    va_s = np.concatenate([np.asarray(R[c]["ovA_s"]).reshape(2, NSTREAM, DEC, 4, 64) for c in range(NCORES)], axis=1)
    kb_s = np.concatenate([np.asarray(R[c]["okB_s"]).transpose(2, 1, 0).reshape(NSTREAM, DEC, 4, 64)
                           for c in range(NCORES)], axis=0)
    vb_s = np.concatenate([np.asarray(R[c]["ovB_s"]).reshape(NSTREAM, DEC, 4, 64) for c in range(NCORES)], axis=0)
    c32 = lambda a: np.ascontiguousarray(a, dtype=np.float32)
    return (c32(y_prompt), c32(y_sample), c32(ka_p), c32(va_p), c32(kb_p), c32(vb_p),
            c32(ka_s), c32(va_s), c32(kb_s), c32(vb_s))
```
